# Optimizing a Trainium2 kernel written in Bass

```python
import jax, jax.numpy as jnp
from jax import lax
import numpy as np

D_MODEL = 1024
BATCH = 32
SEQ = 2048
DEPTH = 2

N_MIXERS = 2
CONV_K = 4
EPS = 1e-6
M_INNER = 2 * D_MODEL
M_HEADS = 4
M_DV = M_INNER // M_HEADS
M_DK = M_DV // 2
M_CHUNK = 64
R_WIDTH = 3 * D_MODEL // 2
R_BLOCKS = 8
R_BS = R_WIDTH // R_BLOCKS
LRU_C = 8.0

N_A = (DEPTH + 1) // 2
N_B = DEPTH // 2

kernel_name = "hybrid_mlstm_rglru_trunk"


def rmsnorm(x, g):
    xf = x.astype(jnp.float32)
    y = xf * lax.rsqrt(jnp.mean(xf * xf, axis=-1, keepdims=True) + EPS)
    return (y * g.astype(jnp.float32)).astype(x.dtype)


def causal_dwconv(x, w, b):
    s = x.shape[1]
    xp = jnp.pad(x, ((0, 0), (CONV_K - 1, 0), (0, 0)))
    y = b + xp[:, 0:s] * w[0]
    for j in range(1, CONV_K):
        y = y + xp[:, j:j + s] * w[j]
    return y


def mlstm_chunkwise(q, k, v, log_i, log_f):
    b, h, s, dk = q.shape
    dv = v.shape[-1]
    L = M_CHUNK
    nc = s // L

    def to_chunks(t):
        t = t.reshape((b, h, nc, L) + t.shape[3:])
        return jnp.moveaxis(t, 2, 0)

    qc, kc, vc, ic, fc = (to_chunks(t) for t in (q, k, v, log_i, log_f))
    causal = jnp.tril(jnp.ones((L, L), dtype=bool))

    def step(carry, inp):
        C, n, m = carry
        qb, kb, vb, ib, fb = inp
        F = jnp.cumsum(fb, axis=-1)
        D = F[..., :, None] - F[..., None, :] + ib[..., None, :]
        D = jnp.where(causal, D, -jnp.inf)
        m_t = jnp.maximum(F + m[..., None], jnp.max(D, axis=-1))
        decay = jnp.exp(F + m[..., None] - m_t)
        W = jnp.exp(D - m_t[..., None])
        Sm = jnp.einsum('bhtk,bhsk->bhts', qb, kb) * W
        num = (decay[..., None] * jnp.einsum('bhtk,bhkv->bhtv', qb, C)
               + jnp.einsum('bhts,bhsv->bhtv', Sm, vb))
        den = decay * jnp.einsum('bhtk,bhk->bht', qb, n) + jnp.sum(Sm, axis=-1)
        hb = num / jnp.maximum(jnp.abs(den), jnp.exp(-m_t))[..., None]
        w_last = W[..., -1, :]
        d_last = decay[..., -1]
        C_new = d_last[..., None, None] * C + jnp.einsum('bhs,bhsk,bhsv->bhkv', w_last, kb, vb)
        n_new = d_last[..., None] * n + jnp.einsum('bhs,bhsk->bhk', w_last, kb)
        return (C_new, n_new, m_t[..., -1]), hb

    init = (jnp.zeros((b, h, dk, dv), jnp.float32),
            jnp.zeros((b, h, dk), jnp.float32),
            jnp.zeros((b, h), jnp.float32))
    _, hs = lax.scan(step, init, (qc, kc, vc, ic, fc))
    return jnp.moveaxis(hs, 0, 2).reshape(b, h, s, dv)


def mlstm_mixer(u, w_in, conv_w, conv_b, w_q, w_k, w_v, b_i, b_f, norm_w, skip, w_out):
    bsz, s, _ = u.shape
    proj = u @ w_in
    xm, z, o_pre, i_pre, f_pre = jnp.split(
        proj, [M_INNER, 2 * M_INNER, 3 * M_INNER, 3 * M_INNER + M_HEADS], axis=-1)
    xc = jax.nn.silu(causal_dwconv(xm, conv_w, conv_b))
    xch = xc.reshape(bsz, s, M_HEADS, M_DV)
    xmh = xm.reshape(bsz, s, M_HEADS, M_DV)
    q = jnp.einsum('bshd,hde->bhse', xch, w_q).astype(jnp.float32) * (M_DK ** -0.5)
    k = jnp.einsum('bshd,hde->bhse', xch, w_k).astype(jnp.float32)
    v = jnp.einsum('bshd,hde->bhse', xmh, w_v).astype(jnp.float32)
    log_i = jnp.transpose((i_pre + b_i).astype(jnp.float32), (0, 2, 1))
    log_f = jnp.transpose(jax.nn.log_sigmoid((f_pre + b_f).astype(jnp.float32)), (0, 2, 1))
    h_tilde = jnp.transpose(mlstm_chunkwise(q, k, v, log_i, log_f), (0, 2, 1, 3))
    o = jax.nn.sigmoid(o_pre.astype(jnp.float32)).reshape(bsz, s, M_HEADS, M_DV)
    hh = o * h_tilde
    mu = jnp.mean(hh, axis=-1, keepdims=True)
    var = jnp.mean(jnp.square(hh - mu), axis=-1, keepdims=True)
    hn = ((hh - mu) * lax.rsqrt(var + EPS)).reshape(bsz, s, M_INNER)
    hn = (hn * norm_w.astype(jnp.float32)).astype(u.dtype)
    y = (hn + skip * xc) * jax.nn.silu(z)
    return y @ w_out


def linear_scan(a, bt):
    def step(hc, ab):
        a_t, b_t = ab
        hc = a_t * hc + b_t
        return hc, hc
    h0 = jnp.zeros((a.shape[0], a.shape[2]), jnp.float32)
    _, hs = lax.scan(step, h0, (jnp.moveaxis(a, 1, 0), jnp.moveaxis(bt, 1, 0)))
    return jnp.moveaxis(hs, 0, 1)


def rglru_mixer(u, w_in, conv_w, conv_b, w_a, b_a, w_x, b_x, lam, w_out):
    bsz, s, _ = u.shape
    proj = u @ w_in
    xr, g = jnp.split(proj, [R_WIDTH], axis=-1)
    xc = causal_dwconv(xr, conv_w, conv_b)
    xb = xc.reshape(bsz, s, R_BLOCKS, R_BS)
    r = jax.nn.sigmoid(jnp.einsum('bsnc,ncd->bsnd', xb, w_a).reshape(bsz, s, R_WIDTH) + b_a)
    ig = jax.nn.sigmoid(jnp.einsum('bsnc,ncd->bsnd', xb, w_x).reshape(bsz, s, R_WIDTH) + b_x)
    log_a = -LRU_C * r.astype(jnp.float32) * jax.nn.softplus(-lam.astype(jnp.float32))
    a = jnp.exp(log_a)
    bt = jnp.sqrt(-jnp.expm1(2.0 * log_a)) * (ig * xc).astype(jnp.float32)
    hs = linear_scan(a, bt).astype(u.dtype)
    y = hs * jax.nn.silu(g)
    return y @ w_out


def setup_inputs(seed: int = 0) -> dict:
    key = jax.random.key(seed)
    ks = iter(jax.random.split(key, 32))
    nrm = lambda shape, scale: jax.random.normal(next(ks), shape, jnp.float32) * scale
    i_w = 3 * M_INNER + 2 * M_HEADS
    lam_u = jax.random.uniform(next(ks), (N_B, R_WIDTH), jnp.float32, 0.9, 0.999)
    sig = lam_u ** (1.0 / LRU_C)
    b_f = jnp.broadcast_to(jnp.linspace(3.0, 6.0, M_HEADS, dtype=jnp.float32), (N_A, M_HEADS))
    return {
        "x": nrm((BATCH, SEQ, D_MODEL), 1.0),
        "ln_g": 1.0 + nrm((DEPTH, D_MODEL), 0.05),
        "final_g": 1.0 + nrm((D_MODEL,), 0.05),
        "m_w_in": nrm((N_A, D_MODEL, i_w), D_MODEL ** -0.5),
        "m_conv_w": nrm((N_A, CONV_K, M_INNER), CONV_K ** -0.5),
        "m_conv_b": nrm((N_A, M_INNER), 0.02),
        "m_w_q": nrm((N_A, M_HEADS, M_DV, M_DK), M_DV ** -0.5),
        "m_w_k": nrm((N_A, M_HEADS, M_DV, M_DK), M_DV ** -0.5),
        "m_w_v": nrm((N_A, M_HEADS, M_DV, M_DV), M_DV ** -0.5),
        "m_b_i": nrm((N_A, M_HEADS), 0.1),
        "m_b_f": b_f + nrm((N_A, M_HEADS), 0.1),
        "m_norm_w": 1.0 + nrm((N_A, M_INNER), 0.05),
        "m_skip": 1.0 + nrm((N_A, M_INNER), 0.05),
        "m_w_out": nrm((N_A, M_INNER, D_MODEL), M_INNER ** -0.5),
        "r_w_in": nrm((N_B, D_MODEL, 2 * R_WIDTH), D_MODEL ** -0.5),
        "r_conv_w": nrm((N_B, CONV_K, R_WIDTH), CONV_K ** -0.5),
        "r_conv_b": nrm((N_B, R_WIDTH), 0.02),
        "r_w_a": nrm((N_B, R_BLOCKS, R_BS, R_BS), R_BS ** -0.5),
        "r_b_a": nrm((N_B, R_WIDTH), 0.02),
        "r_w_x": nrm((N_B, R_BLOCKS, R_BS, R_BS), R_BS ** -0.5),
        "r_b_x": nrm((N_B, R_WIDTH), 0.02),
        "r_lam": jnp.log(sig / (1.0 - sig)),
        "r_w_out": nrm((N_B, R_WIDTH, D_MODEL), R_WIDTH ** -0.5),
    }


def reference(x, ln_g, final_g, m_w_in, m_conv_w, m_conv_b, m_w_q, m_w_k, m_w_v, m_b_i, m_b_f,
              m_norm_w, m_skip, m_w_out, r_w_in, r_conv_w, r_conv_b, r_w_a, r_b_a, r_w_x, r_b_x,
              r_lam, r_w_out):
    for layer in range(DEPTH):
        u = rmsnorm(x, ln_g[layer])
        j = layer // N_MIXERS
        if layer % N_MIXERS == 0:
            y = mlstm_mixer(u, m_w_in[j], m_conv_w[j], m_conv_b[j], m_w_q[j], m_w_k[j], m_w_v[j],
                            m_b_i[j], m_b_f[j], m_norm_w[j], m_skip[j], m_w_out[j])
        else:
            y = rglru_mixer(u, r_w_in[j], r_conv_w[j], r_conv_b[j], r_w_a[j], r_b_a[j],
                            r_w_x[j], r_b_x[j], r_lam[j], r_w_out[j])
        x = x + y
    return rmsnorm(x, final_g)
```

```python
import numpy as np
from contextlib import ExitStack
import concourse.bass as bass
import concourse.mybir as mybir
from concourse.bass_utils import run_bass_kernel_spmd

F32 = mybir.dt.float32
BF16 = mybir.dt.bfloat16
AF = mybir.ActivationFunctionType
ALU = mybir.AluOpType

NCORES = 8
NSEQ = 4
S = 2048
D = 1024
NSUB = 16
T = 512
NT = S // T
EPS = 1e-6
MI = 2048
NH = 4
DV = 512
DK = 256
RW = 1536
NG = 4
GW = 384


class Sched:
    COMPUTE = ("pe", "act", "dve", "pool")
    DEFCOST = {"pe": 0.25, "act": 0.7, "dve": 0.7, "pool": 1.0, "sp": 2.0}
    LAT = 0.25

    def __init__(self, nc, es, ndma=8):
        self.nc = nc
        self.items = {e: [] for e in ("pe", "act", "dve", "pool", "sp")}
        self.sem = {e: es.enter_context(nc.semaphore("sem_" + e)) for e in self.COMPUTE}
        self.cnt = {e: 0 for e in self.COMPUTE}
        self.ndma = ndma
        self.dsem = {q: [es.enter_context(nc.semaphore("dsem_%s%d" % (q, i))) for i in range(ndma)]
                     for q in ("sp", "pool")}
        self.dn = {"sp": 0, "pool": 0}
        self.known = {e: {} for e in self.items}
        self.lastw = {}
        self.readers = {}
        self.semobj = {}
        self._rec = None
        self.efree = {e: 0.0 for e in self.items}
        self.evt = {}

    def record(self, f):
        old = self._rec
        self._rec = []
        f()
        l = self._rec
        self._rec = old
        return l

    def mark(self, name):
        if self._rec is not None:
            self._rec.append(("mark", name))

    def wait(self, name):
        if self._rec is not None:
            self._rec.append(("wait", name))

    def play(self, l):
        for a in l:
            if a[0] in ("mark", "wait"):
                continue
            self._op(*a)

    def schedule(self, streams):
        ptr = [0] * len(streams)
        released = set()
        while True:
            best = None
            progressed = False
            for si, st in enumerate(streams):
                while ptr[si] < len(st) and (st[ptr[si]][0] == "mark" or (st[ptr[si]][0] == "wait" and st[ptr[si]][1] in released)):
                    if st[ptr[si]][0] == "mark":
                        released.add(st[ptr[si]][1])
                    ptr[si] += 1
                    progressed = True
                if ptr[si] >= len(st):
                    continue
                a = st[ptr[si]]
                if a[0] == "wait":
                    continue
                t = self._deps(a[0], a[2], a[3])[1]
                t = max(t, self.efree[a[0]])
                if best is None or t < best[0] - 1e-9:
                    best = (t, si)
            if best is None:
                if progressed:
                    continue
                assert all(ptr[i] >= len(st) for i, st in enumerate(streams)), "schedule deadlock"
                return
            si = best[1]
            self._op(*streams[si][ptr[si]])
            ptr[si] += 1

    def op(self, eng, fn, reads=(), writes=(), dma=False, cost=None):
        if self._rec is not None:
            self._rec.append((eng, fn, tuple(reads), tuple(writes), dma, cost))
            return None
        return self._op(eng, fn, reads, writes, dma, cost)

    def _ev_key(self, sem):
        k = id(sem)
        self.semobj[k] = sem
        return k

    def _deps(self, eng, reads, writes):
        deps = {}
        tmax = [0.0]

        def need(ev):
            if ev is None:
                return
            k, val, prod = ev
            if eng == "pe" and prod == "pe":
                return
            t = self.evt.get((k, val), 0.0) + (0.05 if prod == eng else self.LAT)
            if t > tmax[0]:
                tmax[0] = t
            if deps.get(k, 0) < val:
                deps[k] = val

        for key in reads:
            need(self.lastw.get(key))
        for key in writes:
            lw = self.lastw.get(key)
            if eng == "pe" and lw is not None and lw[2] == "pe" and key.startswith("ps_") and key != "ps_SD":
                raise RuntimeError("PE overwrites un-evacuated PSUM bank " + key)
            need(lw)
            for ev in self.readers.get(key, {}).values():
                need(ev)
        return deps, tmax[0]

    def _op(self, eng, fn, reads=(), writes=(), dma=False, cost=None):
        deps, tready = self._deps(eng, reads, writes)
        if cost is None:
            cost = self.DEFCOST[eng]
        if dma:
            i = self.dn[eng]
            self.dn[eng] += 1
            sem = self.dsem[eng][i % self.ndma]
            val = 16 * (i // self.ndma + 1)
            k = self._ev_key(sem)
            if i >= self.ndma:
                if deps.get(k, 0) < val - 16:
                    deps[k] = val - 16
                tready = max(tready, self.evt.get((k, val - 16), 0.0))
            ev = (k, val, "dma")
            inc = (sem, 16)
            start = max(tready, self.efree[eng])
            self.efree[eng] = start + 0.15
            self.evt[(k, val)] = start + cost
        else:
            self.cnt[eng] += 1
            sem = self.sem[eng]
            k = self._ev_key(sem)
            ev = (k, self.cnt[eng], eng)
            inc = (sem, 1)
            start = max(tready, self.efree[eng])
            self.efree[eng] = start + cost
            self.evt[(k, self.cnt[eng])] = start + cost
        waits = []
        kn = self.known[eng]
        for k, val in deps.items():
            if kn.get(k, 0) < val:
                kn[k] = val
                waits.append((self.semobj[k], val))
        self.items[eng].append((waits, fn, inc))
        for key in writes:
            self.lastw[key] = ev
            self.readers[key] = {}
        for key in reads:
            if key in writes:
                continue
            r = self.readers.setdefault(key, {})
            old = r.get(ev[0])
            if old is None or old[1] < ev[1]:
                r[ev[0]] = ev
        return ev

    def finish(self, eng="sp"):
        waits = []
        for q in ("sp", "pool"):
            n = self.dn[q]
            for j in range(min(n, self.ndma)):
                last_i = ((n - 1 - j) // self.ndma) * self.ndma + j
                waits.append((self.dsem[q][j], 16 * (last_i // self.ndma + 1)))
        for e in self.COMPUTE:
            if self.cnt[e]:
                waits.append((self.sem[e], self.cnt[e]))
        self.items[eng].append((waits, None, None))

    def emit(self, block):
        def run(name):
            def f(e):
                for waits, fn, inc in self.items[name]:
                    for sem, val in waits:
                        e.wait_ge(sem, val)
                    if fn is None:
                        continue
                    ins = fn(e)
                    ins.then_inc(inc[0], inc[1])
            return f
        block.tensor(run("pe"))
        block.scalar(run("act"))
        block.vector(run("dve"))
        block.gpsimd(run("pool"))
        block.sync(run("sp"))


def build_program():
    nc = bass.Bass("TRN2", target_bir_lowering=False)

    def din(name, shape):
        return nc.dram_tensor(name, list(shape), F32, kind="ExternalInput").ap()

    x_d = din("x", [NSEQ, S, D])
    lng_d = din("lng", [128, 16])
    fg_d = din("fg_bc", [128, D])
    mwin_d = din("m_w_in", [D, 3 * MI + 8])
    mwif_d = din("m_w_if", [D, 8])
    mconv_d = din("m_conv", [128, 16 * 5])
    mnw_d = din("m_nw", [128, 16])
    mskip_d = din("m_skip", [128, 16])
    mbif_d = din("m_bif", [4, 2])
    mwq_d = din("m_w_q", [NH, DV, DK])
    mwk_d = din("m_w_k", [NH, DV, DK])
    mwv_d = din("m_w_v", [NH, DV, DV])
    mwout_d = din("m_w_out", [MI, D])
    rwin_d = din("r_w_in", [D, 2 * RW])
    rconv_d = din("r_conv", [128, 12 * 5])
    rba_d = din("r_ba", [128, 12])
    rbx_d = din("r_bx", [128, 12])
    rlam_d = din("r_lam", [128, 12])
    rwa_d = din("r_wa_pad", [NG, GW, GW])
    rwx_d = din("r_wx_pad", [NG, GW, GW])
    rwout_d = din("r_w_out", [RW, D])
    out_d = nc.dram_tensor("out", [NSEQ, S, D], F32, kind="ExternalOutput").ap()

    with ExitStack() as es:
        def sb(name, shape, dt=F32):
            return es.enter_context(nc.sbuf_tensor("s_" + name, list(shape), dt))

        def pbank(name):
            return es.enter_context(nc.psum_tensor(name, [128, 512], F32))

        sc = Sched(nc, es)

        X = sb("X", [128, NSUB, D])
        uT = sb("uT", [128, 8, S], BF16)
        fg = sb("fg", [128, D])
        ident = sb("ident", [128, 128], BF16)
        mask01 = sb("mask01", [128, 128])
        id4 = sb("id4", [4, 4])
        ones4 = sb("ones4", [4, 128])
        onecol = sb("onecol", [128, 1], BF16)
        onecolf = sb("onecolf", [128, 1])
        nhalf = sb("nhalf", [128, 1])
        lng = sb("lng", [128, 16])
        mconv = sb("mconv", [128, 16, 5])
        mnw = sb("mnw", [128, 16])
        mskip = sb("mskip", [128, 16])
        mbif = sb("mbif", [4, 2])
        nbf = sb("nbf", [4, 1])
        rconv = sb("rconv", [128, 12, 5])
        rba = sb("rba", [128, 12])
        rbx = sb("rbx", [128, 12])
        rlam = sb("rlam", [128, 12])
        rc = sb("rc", [128, 12])
        rbah = sb("rbah", [128, 12])
        rbxh = sb("rbxh", [128, 12])
        rch = sb("rch", [128, 12])
        quartcol = sb("quartcol", [128, 1])
        wA = sb("wA", [128, 4096], BF16)
        wB = sb("wB", [128, 4096], BF16)
        wC = sb("wC", [128, 4096], BF16)
        wQ = sb("wQ", [128, 1024], BF16)
        wK = sb("wK", [128, 1024], BF16)
        wV = sb("wV", [128, 2048], BF16)
        wO = sb("wO", [128, 4096], BF16)
        wif = sb("wif", [128, 8, 8], BF16)
        xmf = sb("xmf", [128, 4 * 515])
        xmT = sb("xmT", [128, 4, 512], BF16)
        xcT = sb("xcT", [128, 4, 512], BF16)
        Abf = sb("Abf", [128, 4, 512], BF16)
        xcTb = sb("xcTb", [128, 4, 512], BF16)
        Abfb = sb("Abfb", [128, 4, 512], BF16)
        hhB = sb("hhB", [128, 512])
        osb = sb("osb", [128, 4, 512], BF16)
        qT = sb("qT", [128, 2, 512], BF16)
        kT = sb("kT", [128, 2, 512], BF16)
        kw = sb("kw", [128, 4, 256], BF16)
        vsb = sb("vsb", [128, 4, 512], BF16)
        acc = sb("acc", [128, 512])
        hh = sb("hh", [128, 512])
        SmT = sb("SmT", [128, 128], BF16)
        hn = sb("hn", [128, 512], BF16)
        t1 = sb("t1", [128, 4, 128], BF16)
        yT = sb("yT", [128, 4, 128], BF16)
        Chat = sb("Chat", [128, 2, 512])
        Cbf = sb("Cbf", [128, 2, 512], BF16)
        nhat = sb("nhat", [128, 2])
        nbf16 = sb("nbf16", [128, 2], BF16)
        cols = sb("cols", [128, 3, NSUB, 4])
        dlast = sb("dlast", [128, NSUB + 1, 4])
        Dm = sb("Dm", [4, 4, 4])
        small = sb("small", [128, 32])
        st6x = sb("st6x", [128, 4, 6])
        mvx = sb("mvx", [128, 4, 2])
        ss = sb("ss", [128, NSUB])
        rstd = sb("rstd", [128, NSUB])
        xn = sb("xn", [128, D], BF16)
        outst = sb("outst", [128, D])
        hcar = sb("hcar", [128, 3])

        identf_ap = outst[:, 0:128]
        P = [pbank("P0"), pbank("P1")]
        TB = [pbank("Chat1"), pbank("outA")]
        SD = pbank("SD")
        NB = pbank("NB")
        CB = [pbank("C0"), pbank("C1")]

        rot = {"P": 0, "T": 0, "G": 0}
        XMF = ["xmf0", "xmf1", "xmf2", "xmf3"]

        def nextP():
            i = rot["P"]; rot["P"] ^= 1
            return P[i], "ps_P%d" % i

        def nextT():
            i = rot["T"]; rot["T"] ^= 1
            return TB[i], "ps_T%d" % i

        GB = [(NB, "ps_NB"), (CB[0], "ps_C0"), (CB[1], "ps_C1"), (SD, "ps_SD")]

        def nextG():
            i = rot["G"]; rot["G"] = (i + 1) % 4
            return GB[i]

        def v3(t, c):
            return t[:].rearrange("p (c n) -> p c n", c=c)

        w_xm = v3(wA, 8); w_z = v3(wB, 8); w_o = v3(wC, 8)
        w_q = v3(wQ, 4); w_k = v3(wK, 4); w_v = v3(wV, 4); w_out = v3(wO, 4)
        w_xr = wA[:, 0:8 * GW].rearrange("p (c n) -> p c n", c=8)
        w_g = wB[:, 0:8 * GW].rearrange("p (c n) -> p c n", c=8)
        w_a = wC[:, 0:3 * GW].rearrange("p (c n) -> p c n", c=3)
        w_x = wC[:, 2048:2048 + 3 * GW].rearrange("p (c n) -> p c n", c=3)
        w_rout = wO[:, 0:3 * D].rearrange("p (c n) -> p c n", c=3)
        xm3 = xmf[:].rearrange("p (c n) -> p c n", c=4)

        sc.op("pool", lambda e: e.memset(identf_ap, 0.0), writes=["outA"])
        sc.op("pool", lambda e: e.affine_select(out=identf_ap, in_=identf_ap, pattern=[[-1, 128]],
                                                compare_op=ALU.not_equal, fill=1.0, base=0, channel_multiplier=1),
              reads=["outA"], writes=["outA"])
        sc.op("pool", lambda e: e.tensor_copy(out=ident[:], in_=identf_ap), reads=["outA"], writes=["ident"])
        sc.op("pool", lambda e: e.memset(mask01[:], 1.0), writes=["mask01"])
        sc.op("pool", lambda e: e.affine_select(out=mask01[:], in_=mask01[:], pattern=[[1, 128]],
                                                compare_op=ALU.is_ge, fill=0.0, base=0, channel_multiplier=-1),
              reads=["mask01"], writes=["mask01"])
        sc.op("pool", lambda e: e.memset(id4[:], 0.0), writes=["id4"])
        sc.op("pool", lambda e: e.affine_select(out=id4[:], in_=id4[:], pattern=[[-1, 4]],
                                                compare_op=ALU.not_equal, fill=1.0, base=0, channel_multiplier=1),
              reads=["id4"], writes=["id4"])
        sc.op("pool", lambda e: e.memset(ones4[:], 1.0), writes=["ones4"])
        sc.op("pool", lambda e: e.memset(onecol[:], 1.0), writes=["onecol"])
        sc.op("pool", lambda e: e.memset(onecolf[:], 1.0), writes=["onecolf"])
        sc.op("pool", lambda e: e.memset(nhalf[:], -0.5), writes=["nhalf"])
        sc.op("pool", lambda e: e.memset(dlast[:, 0, :], 0.0), writes=["dlast"])
        sc.op("pool", lambda e: e.memset(Chat[:], 0.0), writes=["Chat0", "Chat1"])
        sc.op("pool", lambda e: e.memset(nhat[:], 0.0), writes=["nhat"])

        def load(dst, src, key):
            sc.op("sp", lambda e: e.dma_start(out=dst, in_=src), writes=[key], dma=True)

        load(fg[:], fg_d[:], "fg")
        load(lng[:], lng_d[:], "lng")
        load(mconv[:].rearrange("p a b -> p (a b)"), mconv_d[:], "mconv")
        load(mnw[:], mnw_d[:], "mnw")
        load(mskip[:], mskip_d[:], "mskip")
        load(mbif[:], mbif_d[:], "mbif")
        load(rconv[:].rearrange("p a b -> p (a b)"), rconv_d[:], "rconv")
        load(rba[:], rba_d[:], "rba")
        load(rbx[:], rbx_d[:], "rbx")
        load(rlam[:], rlam_d[:], "rlam")
        sc.op("pool", lambda e: e.dma_start(out=wif[:], in_=mwif_d.rearrange("(c p) n -> p c n", p=128)),
              writes=["wif"], dma=True)
        sc.op("dve", lambda e: e.tensor_scalar(out=nbf[:], in0=mbif[:, 1:2], scalar1=-1.0, scalar2=None, op0=ALU.mult),
              reads=["mbif"], writes=["nbf"])
        sc.op("act", lambda e: e.activation(out=rc[:], in_=rlam[:], func=AF.Exp, scale=-1.0), reads=["rlam"], writes=["rc"])
        sc.op("act", lambda e: e.activation(out=rc[:], in_=rc[:], func=AF.Ln, bias=onecolf[:], scale=1.0),
              reads=["rc", "onecolf"], writes=["rc"])
        sc.op("dve", lambda e: e.tensor_scalar(out=rc[:], in0=rc[:], scalar1=-8.0, scalar2=None, op0=ALU.mult),
              reads=["rc"], writes=["rc"])
        sc.op("dve", lambda e: e.tensor_scalar(out=rch[:], in0=rc[:], scalar1=0.5, scalar2=None, op0=ALU.mult),
              reads=["rc"], writes=["rch"])
        sc.op("dve", lambda e: e.tensor_scalar(out=rbah[:], in0=rba[:], scalar1=0.5, scalar2=None, op0=ALU.mult),
              reads=["rba"], writes=["rbah"])
        sc.op("dve", lambda e: e.tensor_scalar(out=rbxh[:], in0=rbx[:], scalar1=0.5, scalar2=None, op0=ALU.mult),
              reads=["rbx"], writes=["rbxh"])
        sc.op("pool", lambda e: e.memset(quartcol[:], 0.25), writes=["quartcol"])

        def mm_group(out_ap, pairs, reads, bank_key, extra_writes=()):
            n = len(pairs)

            def fn(e):
                ins = None
                for i, (l, r) in enumerate(pairs):
                    ins = e.matmul(out_ap, l, r, start=(i == 0), stop=(i == n - 1))
                return ins
            cost = sum(max(int(r.shape[-1]), 96) / 2200.0 + 0.01 for (l, r) in pairs)
            return sc.op("pe", fn, reads=reads, writes=[bank_key] + list(extra_writes), cost=cost)

        def prepass_tile(layer, tt):
            for st in range(tt * 4, tt * 4 + 4):
                kx = "X%d" % st
                sc.op("act", lambda e, st=st: e.activation(out=xn[:], in_=X[:, st, :], func=AF.Square,
                                                           accum_out=ss[:, st:st + 1]),
                      reads=[kx], writes=["xn", "ss%d" % st], cost=1.1)
                sc.op("dve", lambda e, st=st: e.tensor_scalar(out=ss[:, st:st + 1], in0=ss[:, st:st + 1], scalar1=1.0 / D,
                                                              scalar2=EPS, op0=ALU.mult, op1=ALU.add),
                      reads=["ss%d" % st], writes=["ss%d" % st], cost=0.2)
                sc.op("pool", lambda e, st=st: e.tensor_tensor(out=rstd[:, st:st + 1], in0=ss[:, st:st + 1], in1=nhalf[:], op=ALU.pow),
                      reads=["ss%d" % st, "nhalf"], writes=["rstd%d" % st], cost=0.5)
                sc.op("act", lambda e, st=st: e.activation(out=xn[:], in_=X[:, st, :], func=AF.Copy, scale=rstd[:, st:st + 1]),
                      reads=[kx, "rstd%d" % st], writes=["xn"], cost=1.1)
                tb, tk = nextT()
                tbv = tb[:].bitcast(BF16).rearrange("p (c n) -> p c n", c=8)

                def tfn(e, tbv=tbv):
                    ins = None
                    for kc in range(8):
                        ins = e.transpose(tbv[:, kc, :], xn[:, kc * 128:(kc + 1) * 128], ident[:])
                    return ins
                sc.op("pe", tfn, reads=["xn", "ident"], writes=[tk], cost=0.6)
                sc.op("dve", lambda e, st=st, tbv=tbv: e.tensor_tensor(
                    out=uT[:, :, st * 128:(st + 1) * 128], in0=tbv,
                    in1=lng[:, layer * 8:(layer + 1) * 8].unsqueeze(2).to_broadcast([128, 8, 128]), op=ALU.mult),
                    reads=["lng"], writes=[tk, "uT%d" % (st // 4)], cost=1.0)
            sc.mark("uTready%d_%d" % (layer, tt))

        def prepass(layer):
            for tt in range(NT):
                prepass_tile(layer, tt)

        def mlstm_gates_prepass():
            Fa_e = xmf[0:4, 0:513]; m_e = xmf[0:4, 515:515 + 513]; a_e = xmf[0:4, 1030:1030 + 513]
            li = xmf[0:4, 1545:1545 + 512]
            lf = hh[0:4, :]
            bb = acc[0:4, :]
            onesr = Chat[0:4, 0, :]
            sc.op("dve", lambda e: e.memset(onesr, 1.0), writes=["Chat0"])
            sc.op("dve", lambda e: e.memset(Fa_e[:, 0:1], 0.0), writes=XMF)
            sc.op("dve", lambda e: e.memset(m_e[:, 0:1], 0.0), writes=XMF)
            sc.op("dve", lambda e: e.memset(a_e[:, 0:1], 0.0), writes=XMF)
            return Fa_e, m_e, a_e, li, lf, bb, onesr

        def mlstm_gates_tile(tt, bufs):
            Fa_e, m_e, a_e, li, lf, bb, onesr = bufs
            sc.wait("uTready0_%d" % tt)
            if True:
                tsl = slice(tt * T, (tt + 1) * T)
                pi, pik = nextP()
                mm_group(pi[0:4, :], [(wif[:, kc, 0:4], uT[:, kc, tsl]) for kc in range(8)], ["wif", "uT%d" % tt], pik)
                pf, pfk = nextP()
                mm_group(pf[0:4, :], [(wif[:, kc, 4:8], uT[:, kc, tsl]) for kc in range(8)], ["wif", "uT%d" % tt], pfk)
                sc.op("act", lambda e, pi=pi: e.activation(out=li, in_=pi[0:4, :], func=AF.Identity, bias=mbif[:, 0:1], scale=1.0),
                      reads=["mbif"], writes=[pik] + XMF)
                sc.op("act", lambda e, pf=pf: e.activation(out=lf, in_=pf[0:4, :], func=AF.Exp, bias=nbf[:], scale=-1.0),
                      reads=["nbf"], writes=[pfk, "hh"])
                sc.op("act", lambda e: e.activation(out=lf, in_=lf, func=AF.Ln, bias=onecolf[0:4, :], scale=1.0),
                      reads=["hh", "onecolf"], writes=["hh"])
                sc.op("dve", lambda e: e.tensor_scalar(out=lf, in0=lf, scalar1=-1.0, scalar2=None, op0=ALU.mult),
                      reads=["hh"], writes=["hh"])
                sc.op("dve", lambda e: e.tensor_tensor_scan(out=Fa_e[:, 1:513], data0=onesr, data1=lf, initial=Fa_e[:, 0:1],
                                                            op0=ALU.mult, op1=ALU.add),
                      reads=["hh", "Chat0"] + XMF, writes=XMF)
                sc.op("dve", lambda e: e.tensor_tensor_scan(out=m_e[:, 1:513], data0=lf, data1=li, initial=m_e[:, 0:1],
                                                            op0=ALU.add, op1=ALU.max),
                      reads=["hh"] + XMF, writes=XMF)
                sc.op("dve", lambda e: e.tensor_tensor(out=a_e[:, 1:513], in0=Fa_e[:, 1:513], in1=m_e[:, 1:513], op=ALU.subtract),
                      reads=XMF, writes=XMF)
                sc.op("dve", lambda e: e.tensor_tensor(out=bb, in0=li, in1=Fa_e[:, 1:513], op=ALU.subtract),
                      reads=XMF, writes=["acc"])
                apv = a_e[:, 0:512].rearrange("k (c t) -> k c t", t=128)[:, :, 0:1].to_broadcast([4, 4, 128])
                sc.op("dve", lambda e: e.tensor_tensor(out=lf.rearrange("k (c t) -> k c t", t=128),
                                                       in0=a_e[:, 1:513].rearrange("k (c t) -> k c t", t=128), in1=apv, op=ALU.subtract),
                      reads=XMF, writes=["hh"])
                sc.op("dve", lambda e: e.tensor_tensor(out=bb.rearrange("k (c t) -> k c t", t=128),
                                                       in0=bb.rearrange("k (c t) -> k c t", t=128), in1=apv, op=ALU.add),
                      reads=XMF + ["acc"], writes=["acc"])
                sc.op("act", lambda e: e.activation(out=lf, in_=lf, func=AF.Exp), reads=["hh"], writes=["hh"])
                sc.op("act", lambda e: e.activation(out=bb, in_=bb, func=AF.Exp), reads=["acc"], writes=["acc"])
                sc.op("act", lambda e: e.activation(out=li, in_=m_e[:, 1:513], func=AF.Exp, scale=-1.0),
                      reads=XMF, writes=XMF)
                sc.op("dve", lambda e: e.tensor_copy(out=Fa_e[:, 0:1], in_=Fa_e[:, 512:513]), reads=XMF, writes=XMF)
                sc.op("dve", lambda e: e.reciprocal(out=Fa_e[:, 1:513], in_=lf), reads=["hh"] + XMF, writes=XMF)
                sc.op("dve", lambda e: e.tensor_tensor(out=li, in0=li, in1=Fa_e[:, 1:513], op=ALU.mult), reads=XMF, writes=XMF)
                rows3 = [lf, bb, li]

                def trf(e):
                    ins = None
                    for q in range(3):
                        for c in range(4):
                            ins = e.matmul(SD[:, q * 16 + c * 4:q * 16 + c * 4 + 4], rows3[q][:, c * 128:(c + 1) * 128], id4[:],
                                           start=True, stop=True)
                    return ins
                sc.op("pe", trf, reads=["hh", "acc", "id4"] + XMF, writes=["ps_SD"])
                sc.op("dve", lambda e, tt=tt: e.tensor_copy(
                    out=cols[:, :, tt * 4:(tt + 1) * 4, :], in_=SD[:, 0:48].rearrange("p (q c h) -> p q c h", q=3, c=4)),
                    writes=["ps_SD", "cols"])
                sc.op("dve", lambda e: e.tensor_tensor(
                    out=Dm[:], in0=lf.rearrange("k (c t) -> k c t", t=128)[:, :, 127:128].to_broadcast([4, 4, 4]),
                    in1=id4[:].unsqueeze(1).to_broadcast([4, 4, 4]), op=ALU.mult),
                    reads=["hh", "id4"], writes=["Dm"])
                mm_group(SD[:, 64:80], [(ones4[:], Dm[:].rearrange("k c h -> k (c h)"))], ["ones4", "Dm"], "ps_SD")
                sc.op("dve", lambda e, tt=tt: e.tensor_copy(
                    out=dlast[:, 1 + tt * 4:1 + (tt + 1) * 4, :], in_=SD[:, 64:80].rearrange("p (c h) -> p c h", c=4)),
                    writes=["ps_SD", "dlast"])
                for r in (m_e, a_e):
                    sc.op("dve", lambda e, r=r: e.tensor_copy(out=r[:, 0:1], in_=r[:, 512:513]), reads=XMF, writes=XMF)

        def mlstm_load(h, which):
            cs = slice(h * 512, (h + 1) * 512)
            L = lambda dst, src, key: sc.op("pool", lambda e: e.dma_start(out=dst, in_=src), writes=[key], dma=True)
            if which == "wA":
                L(w_xm, mwin_d[:, h * 512:(h + 1) * 512].rearrange("(c p) n -> p c n", p=128), "wA")
            elif which == "wB":
                L(w_z, mwin_d[:, MI + h * 512:MI + (h + 1) * 512].rearrange("(c p) n -> p c n", p=128), "wB")
            elif which == "wC":
                L(w_o, mwin_d[:, 2 * MI + h * 512:2 * MI + (h + 1) * 512].rearrange("(c p) n -> p c n", p=128), "wC")
            elif which == "wQ":
                L(w_q, mwq_d[h].rearrange("(c p) n -> p c n", p=128), "wQ")
            elif which == "wK":
                L(w_k, mwk_d[h].rearrange("(c p) n -> p c n", p=128), "wK")
            elif which == "wV":
                sc.op("pool", lambda e: e.dma_start(out=w_v, in_=mwv_d[h].rearrange("(c p) n -> p c n", p=128)), writes=["wV", "wVb"], dma=True)
            elif which == "wO":
                L(w_out, mwout_d[cs, :].rearrange("(c p) n -> p c n", p=128), "wO")

        ob16 = outst[:].bitcast(BF16)
        HN2 = [hn[:], ob16[:, 0:512]]; HNK = ["hn", "hnB"]
        T12 = [t1[:], ob16[:, 512:1024].rearrange("p (c n) -> p c n", c=4)]; T1K = ["t1", "t1B"]
        YT2 = [yT[:], ob16[:, 1024:1536].rearrange("p (c n) -> p c n", c=4)]; YTK = ["yT", "yTB"]
        xcT2 = [xcT, xcTb]; Abf2 = [Abf, Abfb]; hh2 = [hh, hhB]
        XK = ["xcT", "xcTb"]; AK = ["Abf", "AbfB"]; HK = ["hh", "hhB"]

        def m_S2(h, tt, par, pf):
            tsl = slice(tt * T, (tt + 1) * T)
            ukey = "uT%d" % tt
            xcT_ = xcT2[par]; Abf_ = Abf2[par]; xk = XK[par]; ak = AK[par]
            if tt == 0:
                sc.op("pool", lambda e: e.memset(xm3[:, :, 0:3], 0.0), writes=XMF, cost=0.2)
            for fc in range(4):
                pb, pk = nextP()
                mm_group(pb[:], [(w_xm[:, kc, fc * 128:(fc + 1) * 128], uT[:, kc, tsl]) for kc in range(8)], ["wA", ukey], pk)
                sc.op("act", lambda e, pb=pb, fc=fc: e.activation(out=xm3[:, fc, 3:515], in_=pb[:], func=AF.Copy),
                      writes=[pk, "xmf%d" % fc])
                sc.op("act", lambda e, pb=pb, fc=fc: e.activation(out=xmT[:, fc, :], in_=pb[:], func=AF.Copy),
                      writes=[pk, "xmT"])
                cw = lambda j, fc=fc: mconv[:, h * 4 + fc, j:j + 1]
                sc.op("dve", lambda e, fc=fc, cw=cw: e.tensor_scalar(out=acc[:], in0=xm3[:, fc, 0:512], scalar1=cw(0), scalar2=cw(4),
                                                                     op0=ALU.mult, op1=ALU.add),
                      reads=["xmf%d" % fc, "mconv"], writes=["acc"])
                for j in range(1, 4):
                    sc.op("dve", lambda e, fc=fc, j=j, cw=cw: e.scalar_tensor_tensor(
                        out=acc[:], in0=xm3[:, fc, j:j + 512], scalar=cw(j), in1=acc[:], op0=ALU.mult, op1=ALU.add),
                        reads=["xmf%d" % fc, "mconv", "acc"], writes=["acc"])
                sc.op("act", lambda e, fc=fc: e.activation(out=xcT_[:, fc, :], in_=acc[:], func=AF.Silu),
                      reads=["acc"], writes=[xk])
                sc.op("pool", lambda e, fc=fc: e.tensor_copy(out=xm3[:, fc, 0:3], in_=xm3[:, fc, 512:515]),
                      reads=["xmf%d" % fc], writes=["xmf%d" % fc])
            sc.mark("S2xc")
            pf("wA")
            for fc in range(4):
                pb, pk = nextP()
                mm_group(pb[:], [(w_z[:, kc, fc * 128:(fc + 1) * 128], uT[:, kc, tsl]) for kc in range(8)], ["wB", ukey], pk)
                sc.op("act", lambda e, pb=pb, fc=fc: e.activation(out=Abf_[:, fc, :], in_=pb[:], func=AF.Silu), writes=[pk, ak])
            pf("wB")
            sc.wait("osb_free")
            for sub in range(4):
                pb, pk = nextP()
                tk = slice(tt * T + sub * 128, tt * T + (sub + 1) * 128)
                mm_group(pb[:], [(uT[:, kc, tk], w_o[:, kc, :]) for kc in range(8)], ["wC", ukey], pk)
                sc.op("act", lambda e, pb=pb, sub=sub: e.activation(out=osb[:, sub, :], in_=pb[:], func=AF.Sigmoid), writes=[pk, "osb"])
            pf("wC")

        FB = [(NB, "ps_NB"), (CB[0], "ps_C0"), (CB[1], "ps_C1")]
        frot = [0]

        def nextF():
            i = frot[0]; frot[0] = (i + 1) % 3
            return FB[i]

        def m_S3(h, tt, par, pf):
            xcT_ = xcT2[par]; Abf_ = Abf2[par]; xk = XK[par]; ak = AK[par]
            sc.wait("Fdone")
            sc.wait("S2xc")
            for m2 in range(2):
                pb, pk = nextF()
                mm_group(pb[:], [(w_q[:, kc, m2 * 128:(m2 + 1) * 128], xcT_[:, kc, :]) for kc in range(4)], ["wQ", xk], pk)
                sc.op("act", lambda e, pb=pb, m2=m2: e.activation(out=qT[:, m2, :], in_=pb[:], func=AF.Copy, scale=DK ** -0.5),
                      writes=[pk, "qT"])
            for m2 in range(2):
                pb, pk = nextF()
                mm_group(pb[:], [(w_k[:, kc, m2 * 128:(m2 + 1) * 128], xcT_[:, kc, :]) for kc in range(4)], ["wK", xk], pk)
                sc.op("act", lambda e, pb=pb, m2=m2: e.activation(out=kT[:, m2, :], in_=pb[:], func=AF.Copy), writes=[pk, "kT"])
            for sub in range(4):
                pb, pk = nextF()
                mm_group(pb[:], [(xmT[:, kc, sub * 128:(sub + 1) * 128], w_v[:, kc, :]) for kc in range(4)], ["wV", "xmT"], pk)
                sc.op("act", lambda e, pb=pb, sub=sub: e.activation(out=vsb[:, sub, :], in_=pb[:], func=AF.Copy), writes=[pk, "vsb"])
            pf("wQ"); pf("wK"); pf("wV")
            for sub in range(4):
                c = tt * 4 + sub
                tbv = SD[:].bitcast(BF16)[:, 0:256].rearrange("p (c n) -> p c n", c=2)

                def tfn(e, tbv=tbv, sub=sub):
                    ins = None
                    for m2 in range(2):
                        ins = e.transpose(tbv[:, m2, :], kT[:, m2, sub * 128:(sub + 1) * 128], ident[:])
                    return ins
                sc.op("pe", tfn, reads=["kT", "ident"], writes=["ps_SD"], cost=0.15)
                sc.op("dve", lambda e, sub=sub, c=c: e.tensor_scalar(
                    out=kw[:, sub, :], in0=SD[:].bitcast(BF16)[:, 0:256], scalar1=cols[:, 1, c, h:h + 1], scalar2=None, op0=ALU.mult),
                    reads=["cols"], writes=["ps_SD", "kw"], cost=0.35)

        def m_CB(h, tt, par, pf):
            xcT_ = xcT2[par]; Abf_ = Abf2[par]; xk = XK[par]; ak = AK[par]
            def ab_compute():
                for fc in range(4):
                    f = h * 4 + fc
                    sc.op("dve", lambda e, fc=fc, f=f: e.scalar_tensor_tensor(
                        out=xcT_[:, fc, :], in0=xcT_[:, fc, :], scalar=mskip[:, f:f + 1], in1=Abf_[:, fc, :], op0=ALU.mult, op1=ALU.mult),
                        reads=[xk, "mskip", ak], writes=[xk], cost=0.5)
                    sc.op("act", lambda e, fc=fc, f=f: e.activation(out=Abf_[:, fc, :], in_=Abf_[:, fc, :], func=AF.Copy, scale=mnw[:, f:f + 1]),
                          reads=[ak, "mnw"], writes=[ak])

            def front(sub):
                c = tt * 4 + sub
                csl = slice(sub * 128, (sub + 1) * 128)
                hh_ = hh2[sub % 2]; hk = HK[sub % 2]
                dec = cols[:, 0, c, h:h + 1]
                ebc = cols[:, 1, c, h:h + 1]
                emc = cols[:, 2, c, h:h + 1]
                dprev = dlast[:, c, h:h + 1]
                dcur = dlast[:, c + 1, h:h + 1]
                skf = "smallF%d" % sub
                for m2 in range(2):
                    mm_group(CB[m2][:], [(kw[:, sub, m2 * 128:(m2 + 1) * 128], vsb[:, sub, :])], ["kw", "vsb"], "ps_C%d" % m2)
                for m2 in range(2):
                    mm_group(SD[:, 136 + m2:137 + m2], [(kw[:, sub, m2 * 128:(m2 + 1) * 128], onecol[:])], ["kw", "onecol"], "ps_SD")
                for m2 in range(2):
                    sc.op("dve", lambda e, m2=m2, dprev=dprev: e.scalar_tensor_tensor(
                        out=Chat[:, m2, :], in0=Chat[:, m2, :], scalar=dprev, in1=CB[m2][:], op0=ALU.mult, op1=ALU.add),
                        reads=["dlast", "Chat0", "Chat1"], writes=["ps_C%d" % m2, "Chat0", "Chat1"])
                sc.op("dve", lambda e, dprev=dprev: e.scalar_tensor_tensor(
                    out=nhat[:], in0=nhat[:], scalar=dprev, in1=SD[:, 136:138], op0=ALU.mult, op1=ALU.add),
                    reads=["dlast", "nhat"], writes=["ps_SD", "nhat"], cost=0.2)
                mm_group(SD[:, 0:128], [(kT[:, m2, csl], qT[:, m2, csl]) for m2 in range(2)], ["kT", "qT"], "ps_SD")
                sc.op("dve", lambda e, ebc=ebc: e.scalar_tensor_tensor(out=SmT[:], in0=SD[:, 0:128], scalar=ebc, in1=mask01[:],
                                                                       op0=ALU.mult, op1=ALU.mult),
                      reads=["cols", "mask01"], writes=["ps_SD", "SmT"], cost=0.35)
                inter = (c != 0)
                mm_group(NB[:], [(SmT[:], vsb[:, sub, :])] + ([(qT[:, m2, csl], Cbf[:, m2, :]) for m2 in range(2)] if inter else []),
                         ["SmT", "vsb", "qT"] + (["Cbf"] if inter else []), "ps_NB")
                mm_group(SD[:, 128:129], [(SmT[:], onecol[:])] + ([(qT[:, m2, csl], nbf16[:, m2:m2 + 1]) for m2 in range(2)] if inter else []),
                         ["SmT", "onecol", "qT"] + (["nbf16"] if inter else []), "ps_SD")
                sc.op("act", lambda e, dcur=dcur: e.activation(out=Cbf[:].rearrange("p a b -> p (a b)"),
                                                               in_=Chat[:].rearrange("p a b -> p (a b)"), func=AF.Copy, scale=dcur),
                      reads=["Chat0", "Chat1", "dlast"], writes=["Cbf"], cost=1.25)
                sc.op("act", lambda e, dcur=dcur: e.activation(out=nbf16[:], in_=nhat[:], func=AF.Copy, scale=dcur),
                      reads=["nhat", "dlast"], writes=["nbf16"], cost=0.3)
                s0 = small[:, sub * 8 + 0:sub * 8 + 1]; s1 = small[:, sub * 8 + 1:sub * 8 + 2]; s2 = small[:, sub * 8 + 2:sub * 8 + 3]
                sc.op("dve", lambda e: e.tensor_scalar(out=s0, in0=SD[:, 128:129], scalar1=-1.0, scalar2=None, op0=ALU.mult),
                      writes=["ps_SD", skf], cost=0.2)
                sc.op("dve", lambda e, emc=emc: e.scalar_tensor_tensor(out=s1, in0=SD[:, 128:129], scalar=emc, in1=s0, op0=ALU.max, op1=ALU.max),
                      reads=[skf, "cols"], writes=["ps_SD", skf], cost=0.2)
                sc.op("dve", lambda e: e.reciprocal(out=s2, in_=s1), reads=[skf], writes=[skf], cost=0.2)
                sc.op("dve", lambda e, sub=sub, hh_=hh_: e.scalar_tensor_tensor(out=hh_[:], in0=NB[:], scalar=s2, in1=osb[:, sub, :],
                                                                                  op0=ALU.mult, op1=ALU.mult),
                      reads=[skf, "osb"], writes=["ps_NB", hk])

            def back_a(sub):
                hh_ = hh2[sub % 2]; hk = HK[sub % 2]
                hn_ = HN2[sub % 2]; hnk = HNK[sub % 2]
                st6 = st6x[:, sub, :]; mv = mvx[:, sub, :]; skk = "smallK%d" % sub
                sc.op("dve", lambda e: e.bn_stats(out=st6, in_=hh_[:]), reads=[hk], writes=[skk])
                sc.op("dve", lambda e: e.bn_aggr(out=mv, in_=st6), reads=[skk], writes=[skk], cost=0.2)
                s3 = small[:, sub * 8 + 3:sub * 8 + 4]; s4 = small[:, sub * 8 + 4:sub * 8 + 5]; s5 = small[:, sub * 8 + 5:sub * 8 + 6]
                sc.op("pool", lambda e: e.tensor_scalar(out=s3, in0=mv[:, 1:2], scalar1=EPS, scalar2=None, op0=ALU.add),
                      reads=[skk], writes=[skk], cost=0.3)
                sc.op("pool", lambda e: e.tensor_tensor(out=s4, in0=s3, in1=nhalf[:], op=ALU.pow),
                      reads=[skk, "nhalf"], writes=[skk], cost=0.5)
                sc.op("dve", lambda e: e.scalar_tensor_tensor(out=s5, in0=mv[:, 0:1], scalar=-1.0, in1=s4, op0=ALU.mult, op1=ALU.mult),
                      reads=[skk], writes=[skk], cost=0.2)
                sc.op("act", lambda e: e.activation(out=hn_, in_=hh_[:], func=AF.Identity, bias=s5, scale=s4),
                      reads=[hk, skk], writes=[hnk])
                sc.mark("Khn%d" % sub)

            def back_b(sub):
                c = tt * 4 + sub
                csl = slice(sub * 128, (sub + 1) * 128)
                st = c
                hn_ = HN2[sub % 2]; hnk = HNK[sub % 2]
                t1_ = T12[sub % 2]; t1k = T1K[sub % 2]
                yT_ = YT2[sub % 2]; ytk = YTK[sub % 2]
                tb, tk_ = nextT()
                tbv = tb[:].bitcast(BF16)[:, 0:512].rearrange("p (c n) -> p c n", c=4)

                def tfn2(e, tbv=tbv):
                    ins = None
                    for fc in range(4):
                        ins = e.transpose(tbv[:, fc, :], hn_[:, fc * 128:(fc + 1) * 128], ident[:])
                    return ins
                sc.op("pe", tfn2, reads=[hnk, "ident"], writes=[tk_], cost=0.3)
                sc.mark("Ktr%d" % sub)
                sc.op("dve", lambda e, tbv=tbv, csl=csl: e.tensor_tensor(out=t1_, in0=tbv, in1=Abf_[:, :, csl], op=ALU.mult),
                      reads=[ak], writes=[tk_, t1k])
                sc.op("dve", lambda e, csl=csl: e.tensor_tensor(out=yT_, in0=t1_, in1=xcT_[:, :, csl], op=ALU.add),
                      reads=[t1k, xk], writes=[ytk], cost=0.45)
                for half in range(2):
                    pb, pk = nextT()
                    mm_group(pb[:], [(yT_[:, fc, :], w_out[:, fc, half * 512:(half + 1) * 512]) for fc in range(4)], [ytk, "wO"], pk)
                    sc.op("dve", lambda e, pb=pb, st=st, half=half: e.tensor_tensor(
                        out=X[:, st, half * 512:(half + 1) * 512], in0=X[:, st, half * 512:(half + 1) * 512], in1=pb[:], op=ALU.add),
                        reads=["X%d" % st], writes=[pk, "X%d" % st])

            def fstream():
                for sub in range(4):
                    if sub >= 2:
                        sc.wait("Khn%d" % (sub - 2))
                    front(sub)
                    sc.mark("F%d" % sub)
                sc.mark("osb_free")
                sc.mark("Fdone")

            def kastream():
                for sub in range(4):
                    sc.wait("F%d" % sub)
                    if sub >= 2:
                        sc.wait("Ktr%d" % (sub - 2))
                    back_a(sub)

            def kbstream():
                if h == NH - 1 and tt >= 1:
                    prepass_tile(1, tt - 1)
                ab_compute()
                for sub in range(4):
                    sc.wait("Khn%d" % sub)
                    back_b(sub)
                pf("wO")
            return sc.record(fstream), sc.record(kastream), sc.record(kbstream)

        def interleave(A, B):
            out = []
            ia = ib = 0
            na, nb = len(A), len(B)
            while ia < na or ib < nb:
                if ib >= nb or (ia < na and ia * nb <= ib * na):
                    out.append(A[ia]); ia += 1
                else:
                    out.append(B[ib]); ib += 1
            return out

        def rglru_load(g, which):
            L = lambda dst, src, key: sc.op("pool", lambda e: e.dma_start(out=dst, in_=src), writes=[key], dma=True)
            if which == "wA":
                L(w_xr, rwin_d[:, g * GW:(g + 1) * GW].rearrange("(c p) n -> p c n", p=128), "wA")
            elif which == "wB":
                L(w_g, rwin_d[:, RW + g * GW:RW + (g + 1) * GW].rearrange("(c p) n -> p c n", p=128), "wB")
            elif which == "wC":
                L(w_a, rwa_d[g].rearrange("(c p) n -> p c n", p=128), "wC")
                L(w_x, rwx_d[g].rearrange("(c p) n -> p c n", p=128), "wC")
            elif which == "wO":
                L(w_rout, rwout_d[g * GW:(g + 1) * GW, :].rearrange("(c p) n -> p c n", p=128), "wO")

        xr3 = xmf[:, 0:3 * 515].rearrange("p (c n) -> p c n", c=3)
        xcf = [acc, hh, None]
        KV = {0: (0, 1), 1: (0, 1, 2), 2: (1, 2)}

        def f32view(t):
            return t[:].rearrange("p a b -> p (a b)").bitcast(F32)

        RXC32 = [[acc[:], hh[:], Chat[:, 0, :]], [hhB[:], f32view(Cbf), f32view(kw)]]
        RXCK = [["acc", "hh", "Chat0"], ["hhB", "Cbf", "kw"]]
        RXCBF = [xmT, vsb]; RXCBFK = ["xmT", "vsb"]
        RGS = [xcT, xcTb]; RGSK = ["xcT", "xcTb"]

        def r_S2(g, tt, par, pf):
            tsl = slice(tt * T, (tt + 1) * T)
            ukey = "uT%d" % tt
            xc_bf = RXCBF[par]; xcbk = RXCBFK[par]
            gs_bf = RGS[par]; gsk = RGSK[par]
            xc32 = RXC32[par]; xck = RXCK[par]
            if tt == 0:
                sc.op("pool", lambda e: e.memset(xr3[:, :, 0:3], 0.0), writes=XMF, cost=0.2)
            for fc in range(3):
                f = g * 3 + fc
                pb, pk = nextP()
                mm_group(pb[:], [(w_xr[:, kc, fc * 128:(fc + 1) * 128], uT[:, kc, tsl]) for kc in range(8)], ["wA", ukey], pk)
                sc.op("act", lambda e, pb=pb, fc=fc: e.activation(out=xr3[:, fc, 3:515], in_=pb[:], func=AF.Copy),
                      writes=[pk, "xmf%d" % fc])
                cw = lambda j, f=f: rconv[:, f, j:j + 1]
                o32 = xc32[fc]
                sc.op("dve", lambda e, fc=fc, cw=cw, o32=o32: e.tensor_scalar(out=o32, in0=xr3[:, fc, 0:512], scalar1=cw(0), scalar2=cw(4),
                                                                               op0=ALU.mult, op1=ALU.add),
                      reads=["xmf%d" % fc, "rconv"], writes=[xck[fc]])
                for j in range(1, 4):
                    sc.op("dve", lambda e, fc=fc, j=j, cw=cw, o32=o32: e.scalar_tensor_tensor(
                        out=o32, in0=xr3[:, fc, j:j + 512], scalar=cw(j), in1=o32, op0=ALU.mult, op1=ALU.add),
                        reads=["xmf%d" % fc, "rconv", xck[fc]], writes=[xck[fc]])
                sc.op("act", lambda e, fc=fc, o32=o32: e.activation(out=xc_bf[:, fc, :], in_=o32, func=AF.Copy), reads=[xck[fc]], writes=[xcbk])
                sc.op("pool", lambda e, fc=fc: e.tensor_copy(out=xr3[:, fc, 0:3], in_=xr3[:, fc, 512:515]),
                      reads=["xmf%d" % fc], writes=["xmf%d" % fc])
            pf("wA")
            for fc in range(3):
                pb, pk = nextP()
                mm_group(pb[:], [(w_g[:, kc, fc * 128:(fc + 1) * 128], uT[:, kc, tsl]) for kc in range(8)], ["wB", ukey], pk)
                sc.op("act", lambda e, pb=pb, fc=fc: e.activation(out=gs_bf[:, fc, :], in_=pb[:], func=AF.Silu), writes=[pk, gsk])
            pf("wB")

        YB = [Abf, Abfb]; YBK = ["Abf", "AbfB"]

        def r_Ga(g, tt, par, pf):
            xc_bf = RXCBF[par]; xcbk = RXCBFK[par]
            gs_bf = RGS[par]; gsk = RGSK[par]
            xc32 = RXC32[par]; xck = RXCK[par]
            y_bf = YB[par]; ybk = YBK[par]
            if tt == 0:
                sc.op("pool", lambda e: e.memset(hcar[:], 0.0), writes=["hcar"], cost=0.2)
            wVf = wV[:].bitcast(F32)
            if par == 0:
                Ta = [Chat[:, 1, :], f32view(qT), f32view(kT)]
                Tak = ["Chat1", "qT", "kT"]
                Tbs = [(outst[:, 0:512], "outA"), (outst[:, 512:1024], "outB")]
            else:
                Ta = [wQ[:].bitcast(F32), wK[:].bitcast(F32), wVf[:, 0:512]]
                Tak = ["wQ", "wK", "wV"]
                Tbs = [(wVf[:, 512:1024], "wVb"), (xn[:].bitcast(F32), "xn")]
            gbanks = []
            for fc in range(3):
                gb_r, gk_r = nextG()
                mm_group(gb_r[:], [(w_a[:, kc, fc * 128:(fc + 1) * 128], xc_bf[:, kc, :]) for kc in KV[fc]], ["wC", xcbk], gk_r)
                gbanks.append((gb_r, gk_r))
            for fc in range(3):
                f = g * 3 + fc
                gb_r, gk_r = gbanks[fc]
                sc.op("act", lambda e, gb_r=gb_r, f=f, fc=fc: e.activation(out=Ta[fc], in_=gb_r[:], func=AF.Tanh, bias=rbah[:, f:f + 1], scale=0.5),
                      reads=["rbah"], writes=[gk_r, Tak[fc]])
                sc.op("act", lambda e, f=f, fc=fc: e.activation(out=Ta[fc], in_=Ta[fc], func=AF.Exp, scale=rch[:, f:f + 1], bias=rch[:, f:f + 1]),
                      reads=[Tak[fc], "rch"], writes=[Tak[fc]])
            for fc in range(3):
                f = g * 3 + fc
                gb_i, gk_i = nextG()
                mm_group(gb_i[:], [(w_x[:, kc, fc * 128:(fc + 1) * 128], xc_bf[:, kc, :]) for kc in KV[fc]], ["wC", xcbk], gk_i)
                Tb, Tbk = Tbs[fc % 2]
                sc.op("act", lambda e, gb_i=gb_i, f=f, Tb=Tb: e.activation(out=Tb, in_=gb_i[:], func=AF.Tanh, bias=rbxh[:, f:f + 1], scale=0.5),
                      reads=["rbxh"], writes=[gk_i, Tbk])
                sc.op("dve", lambda e, fc=fc, Tb=Tb: e.scalar_tensor_tensor(out=xc32[fc], in0=Tb, scalar=1.0, in1=xc32[fc], op0=ALU.add, op1=ALU.mult),
                      reads=[Tbk, xck[fc]], writes=[xck[fc]])
            pf("wC")
            for fc in range(3):
                Tc, Tck = Tbs[(fc + 1) % 2]
                sc.op("act", lambda e, fc=fc, Tc=Tc: e.activation(out=Tc, in_=Ta[fc], func=AF.Square),
                      reads=[Tak[fc]], writes=[Tck])
                sc.op("act", lambda e, Tc=Tc: e.activation(out=Tc, in_=Tc, func=AF.Sqrt, bias=quartcol[:], scale=-0.25),
                      reads=[Tck, "quartcol"], writes=[Tck])
                sc.op("dve", lambda e, fc=fc, Tc=Tc: e.tensor_tensor(out=xc32[fc], in0=xc32[fc], in1=Tc, op=ALU.mult),
                      reads=[Tck, xck[fc]], writes=[xck[fc]])
                sc.op("dve", lambda e, fc=fc, Tc=Tc: e.tensor_tensor_scan(out=Tc, data0=Ta[fc], data1=xc32[fc], initial=hcar[:, fc:fc + 1],
                                                                          op0=ALU.mult, op1=ALU.add),
                      reads=[Tak[fc], xck[fc], "hcar"], writes=[Tck])
                sc.op("dve", lambda e, fc=fc, Tc=Tc: e.tensor_copy(out=hcar[:, fc:fc + 1], in_=Tc[:, 511:512]), reads=[Tck], writes=["hcar"], cost=0.2)
                sc.op("dve", lambda e, fc=fc, Tc=Tc: e.tensor_tensor(out=y_bf[:, fc, :], in0=Tc, in1=gs_bf[:, fc, :], op=ALU.mult),
                      reads=[Tck, gsk], writes=[ybk])

        def r_Gw(g, tt, par, final, pf):
            y_bf = YB[par]; ybk = YBK[par]
            ostage = f32view(osb)
            for sub in range(4):
                st = tt * 4 + sub
                ssl = slice(sub * 128, (sub + 1) * 128)
                for half in range(2):
                    pb, pk = nextT()
                    mm_group(pb[:], [(y_bf[:, fc, ssl], w_rout[:, fc, half * 512:(half + 1) * 512]) for fc in range(3)], [ybk, "wO"], pk)
                    sc.op("dve", lambda e, pb=pb, st=st, half=half: e.tensor_tensor(
                        out=X[:, st, half * 512:(half + 1) * 512], in0=X[:, st, half * 512:(half + 1) * 512], in1=pb[:], op=ALU.add),
                        reads=["X%d" % st], writes=[pk, "X%d" % st])
                if final is not None:
                    b = final
                    kx = "X%d" % st
                    sc.op("act", lambda e, st=st: e.activation(out=ostage, in_=X[:, st, :], func=AF.Square, accum_out=ss[:, st:st + 1]),
                          reads=[kx], writes=["osb", "ss%d" % st], cost=1.1)
                    sc.op("dve", lambda e, st=st: e.tensor_scalar(out=ss[:, st:st + 1], in0=ss[:, st:st + 1], scalar1=1.0 / D,
                                                                  scalar2=EPS, op0=ALU.mult, op1=ALU.add),
                          reads=["ss%d" % st], writes=["ss%d" % st], cost=0.2)
                    sc.op("pool", lambda e, st=st: e.tensor_tensor(out=rstd[:, st:st + 1], in0=ss[:, st:st + 1], in1=nhalf[:], op=ALU.pow),
                          reads=["ss%d" % st, "nhalf"], writes=["rstd%d" % st], cost=0.5)
                    sc.op("dve", lambda e, st=st: e.scalar_tensor_tensor(out=ostage, in0=X[:, st, :], scalar=rstd[:, st:st + 1],
                                                                         in1=fg[:], op0=ALU.mult, op1=ALU.mult),
                          reads=[kx, "rstd%d" % st, "fg"], writes=["osb"], cost=1.3)
                    sc.op("sp", lambda e, st=st, b=b: e.dma_start(out=out_d[b, st * 128:(st + 1) * 128, :], in_=ostage),
                          reads=["osb"], writes=["outd"], dma=True)
            pf("wO")

        for b in range(NSEQ):
            for st in range(NSUB):
                sc.op("sp", lambda e, b=b, st=st: e.dma_start(out=X[:, st, :], in_=x_d[b, st * 128:(st + 1) * 128, :]),
                      writes=["X%d" % st], dma=True)
            gbufs = mlstm_gates_prepass()
            sA = sc.record(lambda: prepass(0))
            sB = sc.record(lambda: [mlstm_gates_tile(tt, gbufs) for tt in range(NT)])
            sc.schedule([sA, sB])
            for w in ("wA", "wB", "wC", "wQ", "wK", "wV", "wO"):
                mlstm_load(0, w)
            units = [(h, tt) for h in range(NH) for tt in range(NT)]

            def pf_for(h, tt):
                if tt != NT - 1:
                    return lambda w: None
                if h + 1 < NH:
                    return lambda w, h=h: mlstm_load(h + 1, w)
                return lambda w: (rglru_load(0, w) if w in ("wA", "wB", "wC", "wO") else None)

            m_S2(units[0][0], units[0][1], 0, pf_for(*units[0]))
            m_S3(units[0][0], units[0][1], 0, pf_for(*units[0]))
            for i, (h, tt) in enumerate(units):
                par = i % 2
                pfu = pf_for(h, tt)
                fs, kas, kbs = m_CB(h, tt, par, pfu)
                streams = [fs, kas, kbs]
                if i + 1 < len(units):
                    h2, tt2 = units[i + 1]
                    streams.append(sc.record(lambda: m_S2(h2, tt2, 1 - par, pf_for(h2, tt2))))
                    streams.append(sc.record(lambda: m_S3(h2, tt2, 1 - par, pf_for(h2, tt2))))
                sc.schedule(streams)
            unitsR = [(g, tt) for g in range(NG) for tt in range(NT)]

            def pfr_for(g, tt):
                if tt == NT - 1 and g + 1 < NG:
                    return lambda w, g=g: rglru_load(g + 1, w)
                return lambda w: None

            sc.schedule([sc.record(lambda: r_S2(unitsR[0][0], unitsR[0][1], 0, pfr_for(*unitsR[0]))),
                         sc.record(lambda: prepass_tile(1, NT - 1))])
            nR = len(unitsR)
            for i in range(nR + 1):
                streams = []
                if i >= 1:
                    g0, tt0 = unitsR[i - 1]
                    streams.append(sc.record(lambda: r_Gw(g0, tt0, (i - 1) % 2, b if g0 == NG - 1 else None, pfr_for(g0, tt0))))
                if i < nR:
                    g1, tt1 = unitsR[i]
                    streams.append(sc.record(lambda: r_Ga(g1, tt1, i % 2, pfr_for(g1, tt1))))
                if i + 1 < nR:
                    g2, tt2 = unitsR[i + 1]
                    streams.append(sc.record(lambda: r_S2(g2, tt2, (i + 1) % 2, pfr_for(g2, tt2))))
                sc.schedule(streams)
        sc.finish("sp")
        block = es.enter_context(nc.Block())
        sc.emit(block)
    return nc


def _prep_shared(inp):
    f = lambda a: np.ascontiguousarray(np.asarray(a, dtype=np.float32))
    d = {}
    lng = f(inp["ln_g"])
    d["lng"] = f(lng.reshape(2, 8, 128).transpose(2, 0, 1).reshape(128, 16))
    d["fg_bc"] = f(np.broadcast_to(f(inp["final_g"])[None, :], (128, D)))
    mwin = f(inp["m_w_in"][0])
    d["m_w_in"] = mwin
    d["m_w_if"] = f(mwin[:, 3 * MI:3 * MI + 8])
    cw = f(inp["m_conv_w"][0])
    cb = f(inp["m_conv_b"][0])
    mc = np.concatenate([cw, cb[None, :]], 0)
    d["m_conv"] = f(mc.reshape(5, 16, 128).transpose(2, 1, 0).reshape(128, 80))
    d["m_nw"] = f(f(inp["m_norm_w"][0]).reshape(16, 128).T)
    d["m_skip"] = f(f(inp["m_skip"][0]).reshape(16, 128).T)
    d["m_bif"] = f(np.stack([f(inp["m_b_i"][0]), f(inp["m_b_f"][0])], 1))
    d["m_w_q"] = f(inp["m_w_q"][0]); d["m_w_k"] = f(inp["m_w_k"][0]); d["m_w_v"] = f(inp["m_w_v"][0])
    d["m_w_out"] = f(inp["m_w_out"][0])
    d["r_w_in"] = f(inp["r_w_in"][0])
    rw = f(inp["r_conv_w"][0]); rb = f(inp["r_conv_b"][0])
    rcv = np.concatenate([rw, rb[None, :]], 0)
    d["r_conv"] = f(rcv.reshape(5, 12, 128).transpose(2, 1, 0).reshape(128, 60))
    d["r_ba"] = f(f(inp["r_b_a"][0]).reshape(12, 128).T)
    d["r_bx"] = f(f(inp["r_b_x"][0]).reshape(12, 128).T)
    d["r_lam"] = f(f(inp["r_lam"][0]).reshape(12, 128).T)
    for nm, key in (("r_wa_pad", "r_w_a"), ("r_wx_pad", "r_w_x")):
        w = f(inp[key][0])
        pad = np.zeros((NG, GW, GW), np.float32)
        for g in range(NG):
            pad[g, 0:192, 0:192] = w[2 * g]
            pad[g, 192:384, 192:384] = w[2 * g + 1]
        d[nm] = pad
    d["r_w_out"] = f(inp["r_w_out"][0])
    return d


_CACHE = {}


def kernel(**inputs):
    x = np.ascontiguousarray(np.asarray(inputs["x"], dtype=np.float32))
    shared = _prep_shared(inputs)
    if "nc" not in _CACHE:
        _CACHE["nc"] = build_program()
    nc = _CACHE["nc"]
    in_maps = []
    for c in range(NCORES):
        m = dict(shared)
        m["x"] = np.ascontiguousarray(x[c * NSEQ:(c + 1) * NSEQ])
        in_maps.append(m)
    res = run_bass_kernel_spmd(nc, in_maps, core_ids=list(range(NCORES)))
    out = np.concatenate([np.asarray(r["out"], dtype=np.float32) for r in res.results], axis=0)
    return out
```

```python
import numpy as np
from contextlib import ExitStack
import concourse.bass as bass
import concourse.mybir as mybir
from concourse.bass_utils import run_bass_kernel_spmd

F32 = mybir.dt.float32
BF16 = mybir.dt.bfloat16
AF = mybir.ActivationFunctionType
ALU = mybir.AluOpType

NCORES = 8
NSEQ = 4
S = 2048
D = 1024
NSUB = 16
T = 512
NT = S // T
EPS = 1e-6
MI = 2048
NH = 4
DV = 512
DK = 256
RW = 1536
NG = 4
GW = 384


class Sched:
    COMPUTE = ("pe", "act", "dve", "pool")
    DEFCOST = {"pe": 0.25, "act": 0.7, "dve": 0.7, "pool": 1.0, "sp": 2.0}
    LAT = 0.25

    def __init__(self, nc, es, ndma=8):
        self.nc = nc
        self.items = {e: [] for e in ("pe", "act", "dve", "pool", "sp")}
        self.sem = {e: es.enter_context(nc.semaphore("sem_" + e)) for e in self.COMPUTE}
        self.cnt = {e: 0 for e in self.COMPUTE}
        self.ndma = ndma
        self.dsem = {q: [es.enter_context(nc.semaphore("dsem_%s%d" % (q, i))) for i in range(ndma)]
                     for q in ("sp", "pool")}
        self.dn = {"sp": 0, "pool": 0}
        self.known = {e: {} for e in self.items}
        self.lastw = {}
        self.readers = {}
        self.semobj = {}
        self._rec = None
        self.efree = {e: 0.0 for e in self.items}
        self.evt = {}

    def record(self, f):
        old = self._rec
        self._rec = []
        f()
        l = self._rec
        self._rec = old
        return l

    def mark(self, name):
        if self._rec is not None:
            self._rec.append(("mark", name))

    def wait(self, name):
        if self._rec is not None:
            self._rec.append(("wait", name))

    def play(self, l):
        for a in l:
            if a[0] in ("mark", "wait"):
                continue
            self._op(*a)

    def schedule(self, streams):
        ptr = [0] * len(streams)
        released = set()
        while True:
            best = None
            progressed = False
            for si, st in enumerate(streams):
                while ptr[si] < len(st) and (st[ptr[si]][0] == "mark" or (st[ptr[si]][0] == "wait" and st[ptr[si]][1] in released)):
                    if st[ptr[si]][0] == "mark":
                        released.add(st[ptr[si]][1])
                    ptr[si] += 1
                    progressed = True
                if ptr[si] >= len(st):
                    continue
                a = st[ptr[si]]
                if a[0] == "wait":
                    continue
                t = self._deps(a[0], a[2], a[3])[1]
                t = max(t, self.efree[a[0]])
                if best is None or t < best[0] - 1e-9:
                    best = (t, si)
            if best is None:
                if progressed:
                    continue
                assert all(ptr[i] >= len(st) for i, st in enumerate(streams)), "schedule deadlock"
                return
            si = best[1]
            self._op(*streams[si][ptr[si]])
            ptr[si] += 1

    def op(self, eng, fn, reads=(), writes=(), dma=False, cost=None):
        if self._rec is not None:
            self._rec.append((eng, fn, tuple(reads), tuple(writes), dma, cost))
            return None
        return self._op(eng, fn, reads, writes, dma, cost)

    def _ev_key(self, sem):
        k = id(sem)
        self.semobj[k] = sem
        return k

    def _deps(self, eng, reads, writes):
        deps = {}
        tmax = [0.0]

        def need(ev):
            if ev is None:
                return
            k, val, prod = ev
            if eng == "pe" and prod == "pe":
                return
            t = self.evt.get((k, val), 0.0) + (0.05 if prod == eng else self.LAT)
            if t > tmax[0]:
                tmax[0] = t
            if deps.get(k, 0) < val:
                deps[k] = val

        for key in reads:
            need(self.lastw.get(key))
        for key in writes:
            lw = self.lastw.get(key)
            if eng == "pe" and lw is not None and lw[2] == "pe" and key.startswith("ps_") and key != "ps_SD":
                raise RuntimeError("PE overwrites un-evacuated PSUM bank " + key)
            need(lw)
            for ev in self.readers.get(key, {}).values():
                need(ev)
        return deps, tmax[0]

    def _op(self, eng, fn, reads=(), writes=(), dma=False, cost=None):
        deps, tready = self._deps(eng, reads, writes)
        if cost is None:
            cost = self.DEFCOST[eng]
        if dma:
            i = self.dn[eng]
            self.dn[eng] += 1
            sem = self.dsem[eng][i % self.ndma]
            val = 16 * (i // self.ndma + 1)
            k = self._ev_key(sem)
            if i >= self.ndma:
                if deps.get(k, 0) < val - 16:
                    deps[k] = val - 16
                tready = max(tready, self.evt.get((k, val - 16), 0.0))
            ev = (k, val, "dma")
            inc = (sem, 16)
            start = max(tready, self.efree[eng])
            self.efree[eng] = start + 0.15
            self.evt[(k, val)] = start + cost
        else:
            self.cnt[eng] += 1
            sem = self.sem[eng]
            k = self._ev_key(sem)
            ev = (k, self.cnt[eng], eng)
            inc = (sem, 1)
            start = max(tready, self.efree[eng])
            self.efree[eng] = start + cost
            self.evt[(k, self.cnt[eng])] = start + cost
        waits = []
        kn = self.known[eng]
        for k, val in deps.items():
            if kn.get(k, 0) < val:
                kn[k] = val
                waits.append((self.semobj[k], val))
        self.items[eng].append((waits, fn, inc))
        for key in writes:
            self.lastw[key] = ev
            self.readers[key] = {}
        for key in reads:
            if key in writes:
                continue
            r = self.readers.setdefault(key, {})
            old = r.get(ev[0])
            if old is None or old[1] < ev[1]:
                r[ev[0]] = ev
        return ev

    def finish(self, eng="sp"):
        waits = []
        for q in ("sp", "pool"):
            n = self.dn[q]
            for j in range(min(n, self.ndma)):
                last_i = ((n - 1 - j) // self.ndma) * self.ndma + j
                waits.append((self.dsem[q][j], 16 * (last_i // self.ndma + 1)))
        for e in self.COMPUTE:
            if self.cnt[e]:
                waits.append((self.sem[e], self.cnt[e]))
        self.items[eng].append((waits, None, None))

    def emit(self, block):
        def run(name):
            def f(e):
                for waits, fn, inc in self.items[name]:
                    for sem, val in waits:
                        e.wait_ge(sem, val)
                    if fn is None:
                        continue
                    ins = fn(e)
                    ins.then_inc(inc[0], inc[1])
            return f
        block.tensor(run("pe"))
        block.scalar(run("act"))
        block.vector(run("dve"))
        block.gpsimd(run("pool"))
        block.sync(run("sp"))


def build_program():
    nc = bass.Bass("TRN2", target_bir_lowering=False)

    def din(name, shape):
        return nc.dram_tensor(name, list(shape), F32, kind="ExternalInput").ap()

    x_d = din("x", [NSEQ, S, D])
    lng_d = din("lng", [128, 16])
    fg_d = din("fg_bc", [128, D])
    mwin_d = din("m_w_in", [D, 3 * MI + 8])
    mwif_d = din("m_w_if", [D, 8])
    mconv_d = din("m_conv", [128, 16 * 5])
    mnw_d = din("m_nw", [128, 16])
    mskip_d = din("m_skip", [128, 16])
    mbif_d = din("m_bif", [4, 2])
    mwq_d = din("m_w_q", [NH, DV, DK])
    mwk_d = din("m_w_k", [NH, DV, DK])
    mwv_d = din("m_w_v", [NH, DV, DV])
    mwout_d = din("m_w_out", [MI, D])
    rwin_d = din("r_w_in", [D, 2 * RW])
    rconv_d = din("r_conv", [128, 12 * 5])
    rba_d = din("r_ba", [128, 12])
    rbx_d = din("r_bx", [128, 12])
    rlam_d = din("r_lam", [128, 12])
    rwa_d = din("r_wa_pad", [NG, GW, GW])
    rwx_d = din("r_wx_pad", [NG, GW, GW])
    rwout_d = din("r_w_out", [RW, D])
    out_d = nc.dram_tensor("out", [NSEQ, S, D], F32, kind="ExternalOutput").ap()

    with ExitStack() as es:
        def sb(name, shape, dt=F32):
            return es.enter_context(nc.sbuf_tensor("s_" + name, list(shape), dt))

        def pbank(name):
            return es.enter_context(nc.psum_tensor(name, [128, 512], F32))

        sc = Sched(nc, es)

        X = sb("X", [128, NSUB, D])
        uT = sb("uT", [128, 8, S], BF16)
        fg = sb("fg", [128, D])
        ident = sb("ident", [128, 128], BF16)
        mask01 = sb("mask01", [128, 128])
        id4 = sb("id4", [4, 4])
        ones4 = sb("ones4", [4, 128])
        onecol = sb("onecol", [128, 1], BF16)
        onecolf = sb("onecolf", [128, 1])
        nhalf = sb("nhalf", [128, 1])
        lng = sb("lng", [128, 16])
        mconv = sb("mconv", [128, 16, 5])
        mnw = sb("mnw", [128, 16])
        mskip = sb("mskip", [128, 16])
        mbif = sb("mbif", [4, 2])
        nbf = sb("nbf", [4, 1])
        rconv = sb("rconv", [128, 12, 5])
        rba = sb("rba", [128, 12])
        rbx = sb("rbx", [128, 12])
        rlam = sb("rlam", [128, 12])
        rc = sb("rc", [128, 12])
        rbah = sb("rbah", [128, 12])
        rbxh = sb("rbxh", [128, 12])
        rch = sb("rch", [128, 12])
        quartcol = sb("quartcol", [128, 1])
        wA = sb("wA", [128, 4096], BF16)
        wB = sb("wB", [128, 4096], BF16)
        wC = sb("wC", [128, 4096], BF16)
        wQ = sb("wQ", [128, 1024], BF16)
        wK = sb("wK", [128, 1024], BF16)
        wV = sb("wV", [128, 2048], BF16)
        wO = sb("wO", [128, 4096], BF16)
        wif = sb("wif", [128, 8, 8], BF16)
        xmf = sb("xmf", [128, 4 * 515])
        xmT = sb("xmT", [128, 4, 512], BF16)
        xcT = sb("xcT", [128, 4, 512], BF16)
        Abf = sb("Abf", [128, 4, 512], BF16)
        xcTb = sb("xcTb", [128, 4, 512], BF16)
        Abfb = sb("Abfb", [128, 4, 512], BF16)
        hhB = sb("hhB", [128, 512])
        osb = sb("osb", [128, 4, 512], BF16)
        qT = sb("qT", [128, 2, 512], BF16)
        kT = sb("kT", [128, 2, 512], BF16)
        kw = sb("kw", [128, 4, 256], BF16)
        vsb = sb("vsb", [128, 4, 512], BF16)
        acc = sb("acc", [128, 512])
        hh = sb("hh", [128, 512])
        SmT = sb("SmT", [128, 128], BF16)
        hn = sb("hn", [128, 512], BF16)
        t1 = sb("t1", [128, 4, 128], BF16)
        yT = sb("yT", [128, 4, 128], BF16)
        Chat = sb("Chat", [128, 2, 512])
        Cbf = sb("Cbf", [128, 2, 512], BF16)
        nhat = sb("nhat", [128, 2])
        nbf16 = sb("nbf16", [128, 2], BF16)
        cols = sb("cols", [128, 3, NSUB, 4])
        dlast = sb("dlast", [128, NSUB + 1, 4])
        Dm = sb("Dm", [4, 4, 4])
        small = sb("small", [128, 32])
        st6x = sb("st6x", [128, 4, 6])
        mvx = sb("mvx", [128, 4, 2])
        ss = sb("ss", [128, NSUB])
        rstd = sb("rstd", [128, NSUB])
        xn = sb("xn", [128, D], BF16)
        outst = sb("outst", [128, D])
        hcar = sb("hcar", [128, 3])

        identf_ap = outst[:, 0:128]
        P = [pbank("P0"), pbank("P1")]
        TB = [pbank("Chat1"), pbank("outA")]
        SD = pbank("SD")
        NB = pbank("NB")
        CB = [pbank("C0"), pbank("C1")]

        rot = {"P": 0, "T": 0, "G": 0}
        XMF = ["xmf0", "xmf1", "xmf2", "xmf3"]

        def nextP():
            i = rot["P"]; rot["P"] ^= 1
            return P[i], "ps_P%d" % i

        def nextT():
            i = rot["T"]; rot["T"] ^= 1
            return TB[i], "ps_T%d" % i

        GB = [(NB, "ps_NB"), (CB[0], "ps_C0"), (CB[1], "ps_C1"), (SD, "ps_SD")]

        def nextG():
            i = rot["G"]; rot["G"] = (i + 1) % 4
            return GB[i]

        def v3(t, c):
            return t[:].rearrange("p (c n) -> p c n", c=c)

        w_xm = v3(wA, 8); w_z = v3(wB, 8); w_o = v3(wC, 8)
        w_q = v3(wQ, 4); w_k = v3(wK, 4); w_v = v3(wV, 4); w_out = v3(wO, 4)
        w_xr = wA[:, 0:8 * GW].rearrange("p (c n) -> p c n", c=8)
        w_g = wB[:, 0:8 * GW].rearrange("p (c n) -> p c n", c=8)
        w_a = wC[:, 0:3 * GW].rearrange("p (c n) -> p c n", c=3)
        w_x = wC[:, 2048:2048 + 3 * GW].rearrange("p (c n) -> p c n", c=3)
        w_rout = wO[:, 0:3 * D].rearrange("p (c n) -> p c n", c=3)
        xm3 = xmf[:].rearrange("p (c n) -> p c n", c=4)

        sc.op("pool", lambda e: e.memset(identf_ap, 0.0), writes=["outA"])
        sc.op("pool", lambda e: e.affine_select(out=identf_ap, in_=identf_ap, pattern=[[-1, 128]],
                                                compare_op=ALU.not_equal, fill=1.0, base=0, channel_multiplier=1),
              reads=["outA"], writes=["outA"])
        sc.op("pool", lambda e: e.tensor_copy(out=ident[:], in_=identf_ap), reads=["outA"], writes=["ident"])
        sc.op("pool", lambda e: e.memset(mask01[:], 1.0), writes=["mask01"])
        sc.op("pool", lambda e: e.affine_select(out=mask01[:], in_=mask01[:], pattern=[[1, 128]],
                                                compare_op=ALU.is_ge, fill=0.0, base=0, channel_multiplier=-1),
              reads=["mask01"], writes=["mask01"])
        sc.op("pool", lambda e: e.memset(id4[:], 0.0), writes=["id4"])
        sc.op("pool", lambda e: e.affine_select(out=id4[:], in_=id4[:], pattern=[[-1, 4]],
                                                compare_op=ALU.not_equal, fill=1.0, base=0, channel_multiplier=1),
              reads=["id4"], writes=["id4"])
        sc.op("pool", lambda e: e.memset(ones4[:], 1.0), writes=["ones4"])
        sc.op("pool", lambda e: e.memset(onecol[:], 1.0), writes=["onecol"])
        sc.op("pool", lambda e: e.memset(onecolf[:], 1.0), writes=["onecolf"])
        sc.op("pool", lambda e: e.memset(nhalf[:], -0.5), writes=["nhalf"])
        sc.op("pool", lambda e: e.memset(dlast[:, 0, :], 0.0), writes=["dlast"])
        sc.op("pool", lambda e: e.memset(Chat[:], 0.0), writes=["Chat0", "Chat1"])
        sc.op("pool", lambda e: e.memset(nhat[:], 0.0), writes=["nhat"])

        def load(dst, src, key):
            sc.op("sp", lambda e: e.dma_start(out=dst, in_=src), writes=[key], dma=True)

        load(fg[:], fg_d[:], "fg")
        load(lng[:], lng_d[:], "lng")
        load(mconv[:].rearrange("p a b -> p (a b)"), mconv_d[:], "mconv")
        load(mnw[:], mnw_d[:], "mnw")
        load(mskip[:], mskip_d[:], "mskip")
        load(mbif[:], mbif_d[:], "mbif")
        load(rconv[:].rearrange("p a b -> p (a b)"), rconv_d[:], "rconv")
        load(rba[:], rba_d[:], "rba")
        load(rbx[:], rbx_d[:], "rbx")
        load(rlam[:], rlam_d[:], "rlam")
        sc.op("pool", lambda e: e.dma_start(out=wif[:], in_=mwif_d.rearrange("(c p) n -> p c n", p=128)),
              writes=["wif"], dma=True)
        sc.op("dve", lambda e: e.tensor_scalar(out=nbf[:], in0=mbif[:, 1:2], scalar1=-1.0, scalar2=None, op0=ALU.mult),
              reads=["mbif"], writes=["nbf"])
        sc.op("act", lambda e: e.activation(out=rc[:], in_=rlam[:], func=AF.Exp, scale=-1.0), reads=["rlam"], writes=["rc"])
        sc.op("act", lambda e: e.activation(out=rc[:], in_=rc[:], func=AF.Ln, bias=onecolf[:], scale=1.0),
              reads=["rc", "onecolf"], writes=["rc"])
        sc.op("dve", lambda e: e.tensor_scalar(out=rc[:], in0=rc[:], scalar1=-8.0, scalar2=None, op0=ALU.mult),
              reads=["rc"], writes=["rc"])
        sc.op("dve", lambda e: e.tensor_scalar(out=rch[:], in0=rc[:], scalar1=0.5, scalar2=None, op0=ALU.mult),
              reads=["rc"], writes=["rch"])
        sc.op("dve", lambda e: e.tensor_scalar(out=rbah[:], in0=rba[:], scalar1=0.5, scalar2=None, op0=ALU.mult),
              reads=["rba"], writes=["rbah"])
        sc.op("dve", lambda e: e.tensor_scalar(out=rbxh[:], in0=rbx[:], scalar1=0.5, scalar2=None, op0=ALU.mult),
              reads=["rbx"], writes=["rbxh"])
        sc.op("pool", lambda e: e.memset(quartcol[:], 0.25), writes=["quartcol"])

        def mm_group(out_ap, pairs, reads, bank_key, extra_writes=()):
            n = len(pairs)

            def fn(e):
                ins = None
                for i, (l, r) in enumerate(pairs):
                    ins = e.matmul(out_ap, l, r, start=(i == 0), stop=(i == n - 1))
                return ins
            cost = sum(max(int(r.shape[-1]), 96) / 2200.0 + 0.01 for (l, r) in pairs)
            return sc.op("pe", fn, reads=reads, writes=[bank_key] + list(extra_writes), cost=cost)

        def prepass_tile(layer, tt):
            for st in range(tt * 4, tt * 4 + 4):
                kx = "X%d" % st
                sc.op("act", lambda e, st=st: e.activation(out=xn[:], in_=X[:, st, :], func=AF.Square,
                                                           accum_out=ss[:, st:st + 1]),
                      reads=[kx], writes=["xn", "ss%d" % st], cost=1.1)
                sc.op("dve", lambda e, st=st: e.tensor_scalar(out=ss[:, st:st + 1], in0=ss[:, st:st + 1], scalar1=1.0 / D,
                                                              scalar2=EPS, op0=ALU.mult, op1=ALU.add),
                      reads=["ss%d" % st], writes=["ss%d" % st], cost=0.2)
                sc.op("pool", lambda e, st=st: e.tensor_tensor(out=rstd[:, st:st + 1], in0=ss[:, st:st + 1], in1=nhalf[:], op=ALU.pow),
                      reads=["ss%d" % st, "nhalf"], writes=["rstd%d" % st], cost=0.5)
                sc.op("act", lambda e, st=st: e.activation(out=xn[:], in_=X[:, st, :], func=AF.Copy, scale=rstd[:, st:st + 1]),
                      reads=[kx, "rstd%d" % st], writes=["xn"], cost=1.1)
                tb, tk = nextT()
                tbv = tb[:].bitcast(BF16).rearrange("p (c n) -> p c n", c=8)

                def tfn(e, tbv=tbv):
                    ins = None
                    for kc in range(8):
                        ins = e.transpose(tbv[:, kc, :], xn[:, kc * 128:(kc + 1) * 128], ident[:])
                    return ins
                sc.op("pe", tfn, reads=["xn", "ident"], writes=[tk], cost=0.6)
                sc.op("dve", lambda e, st=st, tbv=tbv: e.tensor_tensor(
                    out=uT[:, :, st * 128:(st + 1) * 128], in0=tbv,
                    in1=lng[:, layer * 8:(layer + 1) * 8].unsqueeze(2).to_broadcast([128, 8, 128]), op=ALU.mult),
                    reads=["lng"], writes=[tk, "uT%d" % (st // 4)], cost=1.0)
            sc.mark("uTready%d_%d" % (layer, tt))

        def prepass(layer):
            for tt in range(NT):
                prepass_tile(layer, tt)

        def mlstm_gates_prepass():
            Fa_e = xmf[0:4, 0:513]; m_e = xmf[0:4, 515:515 + 513]; a_e = xmf[0:4, 1030:1030 + 513]
            li = xmf[0:4, 1545:1545 + 512]
            lf = hh[0:4, :]
            bb = acc[0:4, :]
            onesr = Chat[0:4, 0, :]
            sc.op("dve", lambda e: e.memset(onesr, 1.0), writes=["Chat0"])
            sc.op("dve", lambda e: e.memset(Fa_e[:, 0:1], 0.0), writes=XMF)
            sc.op("dve", lambda e: e.memset(m_e[:, 0:1], 0.0), writes=XMF)
            sc.op("dve", lambda e: e.memset(a_e[:, 0:1], 0.0), writes=XMF)
            return Fa_e, m_e, a_e, li, lf, bb, onesr

        def mlstm_gates_tile(tt, bufs):
            Fa_e, m_e, a_e, li, lf, bb, onesr = bufs
            sc.wait("uTready0_%d" % tt)
            if True:
                tsl = slice(tt * T, (tt + 1) * T)
                pi, pik = nextP()
                mm_group(pi[0:4, :], [(wif[:, kc, 0:4], uT[:, kc, tsl]) for kc in range(8)], ["wif", "uT%d" % tt], pik)
                pf, pfk = nextP()
                mm_group(pf[0:4, :], [(wif[:, kc, 4:8], uT[:, kc, tsl]) for kc in range(8)], ["wif", "uT%d" % tt], pfk)
                sc.op("act", lambda e, pi=pi: e.activation(out=li, in_=pi[0:4, :], func=AF.Identity, bias=mbif[:, 0:1], scale=1.0),
                      reads=["mbif"], writes=[pik] + XMF)
                sc.op("act", lambda e, pf=pf: e.activation(out=lf, in_=pf[0:4, :], func=AF.Exp, bias=nbf[:], scale=-1.0),
                      reads=["nbf"], writes=[pfk, "hh"])
                sc.op("act", lambda e: e.activation(out=lf, in_=lf, func=AF.Ln, bias=onecolf[0:4, :], scale=1.0),
                      reads=["hh", "onecolf"], writes=["hh"])
                sc.op("dve", lambda e: e.tensor_scalar(out=lf, in0=lf, scalar1=-1.0, scalar2=None, op0=ALU.mult),
                      reads=["hh"], writes=["hh"])
                sc.op("dve", lambda e: e.tensor_tensor_scan(out=Fa_e[:, 1:513], data0=onesr, data1=lf, initial=Fa_e[:, 0:1],
                                                            op0=ALU.mult, op1=ALU.add),
                      reads=["hh", "Chat0"] + XMF, writes=XMF)
                sc.op("dve", lambda e: e.tensor_tensor_scan(out=m_e[:, 1:513], data0=lf, data1=li, initial=m_e[:, 0:1],
                                                            op0=ALU.add, op1=ALU.max),
                      reads=["hh"] + XMF, writes=XMF)
                sc.op("dve", lambda e: e.tensor_tensor(out=a_e[:, 1:513], in0=Fa_e[:, 1:513], in1=m_e[:, 1:513], op=ALU.subtract),
                      reads=XMF, writes=XMF)
                sc.op("dve", lambda e: e.tensor_tensor(out=bb, in0=li, in1=Fa_e[:, 1:513], op=ALU.subtract),
                      reads=XMF, writes=["acc"])
                apv = a_e[:, 0:512].rearrange("k (c t) -> k c t", t=128)[:, :, 0:1].to_broadcast([4, 4, 128])
                sc.op("dve", lambda e: e.tensor_tensor(out=lf.rearrange("k (c t) -> k c t", t=128),
                                                       in0=a_e[:, 1:513].rearrange("k (c t) -> k c t", t=128), in1=apv, op=ALU.subtract),
                      reads=XMF, writes=["hh"])
                sc.op("dve", lambda e: e.tensor_tensor(out=bb.rearrange("k (c t) -> k c t", t=128),
                                                       in0=bb.rearrange("k (c t) -> k c t", t=128), in1=apv, op=ALU.add),
                      reads=XMF + ["acc"], writes=["acc"])
                sc.op("act", lambda e: e.activation(out=lf, in_=lf, func=AF.Exp), reads=["hh"], writes=["hh"])
                sc.op("act", lambda e: e.activation(out=bb, in_=bb, func=AF.Exp), reads=["acc"], writes=["acc"])
                sc.op("act", lambda e: e.activation(out=li, in_=m_e[:, 1:513], func=AF.Exp, scale=-1.0),
                      reads=XMF, writes=XMF)
                sc.op("dve", lambda e: e.tensor_copy(out=Fa_e[:, 0:1], in_=Fa_e[:, 512:513]), reads=XMF, writes=XMF)
                sc.op("dve", lambda e: e.reciprocal(out=Fa_e[:, 1:513], in_=lf), reads=["hh"] + XMF, writes=XMF)
                sc.op("dve", lambda e: e.tensor_tensor(out=li, in0=li, in1=Fa_e[:, 1:513], op=ALU.mult), reads=XMF, writes=XMF)
                rows3 = [lf, bb, li]

                def trf(e):
                    ins = None
                    for q in range(3):
                        for c in range(4):
                            ins = e.matmul(SD[:, q * 16 + c * 4:q * 16 + c * 4 + 4], rows3[q][:, c * 128:(c + 1) * 128], id4[:],
                                           start=True, stop=True)
                    return ins
                sc.op("pe", trf, reads=["hh", "acc", "id4"] + XMF, writes=["ps_SD"])
                sc.op("dve", lambda e, tt=tt: e.tensor_copy(
                    out=cols[:, :, tt * 4:(tt + 1) * 4, :], in_=SD[:, 0:48].rearrange("p (q c h) -> p q c h", q=3, c=4)),
                    writes=["ps_SD", "cols"])
                sc.op("dve", lambda e: e.tensor_tensor(
                    out=Dm[:], in0=lf.rearrange("k (c t) -> k c t", t=128)[:, :, 127:128].to_broadcast([4, 4, 4]),
                    in1=id4[:].unsqueeze(1).to_broadcast([4, 4, 4]), op=ALU.mult),
                    reads=["hh", "id4"], writes=["Dm"])
                mm_group(SD[:, 64:80], [(ones4[:], Dm[:].rearrange("k c h -> k (c h)"))], ["ones4", "Dm"], "ps_SD")
                sc.op("dve", lambda e, tt=tt: e.tensor_copy(
                    out=dlast[:, 1 + tt * 4:1 + (tt + 1) * 4, :], in_=SD[:, 64:80].rearrange("p (c h) -> p c h", c=4)),
                    writes=["ps_SD", "dlast"])
                for r in (m_e, a_e):
                    sc.op("dve", lambda e, r=r: e.tensor_copy(out=r[:, 0:1], in_=r[:, 512:513]), reads=XMF, writes=XMF)

        def mlstm_load(h, which):
            cs = slice(h * 512, (h + 1) * 512)
            L = lambda dst, src, key: sc.op("pool", lambda e: e.dma_start(out=dst, in_=src), writes=[key], dma=True)
            if which == "wA":
                L(w_xm, mwin_d[:, h * 512:(h + 1) * 512].rearrange("(c p) n -> p c n", p=128), "wA")
            elif which == "wB":
                L(w_z, mwin_d[:, MI + h * 512:MI + (h + 1) * 512].rearrange("(c p) n -> p c n", p=128), "wB")
            elif which == "wC":
                L(w_o, mwin_d[:, 2 * MI + h * 512:2 * MI + (h + 1) * 512].rearrange("(c p) n -> p c n", p=128), "wC")
            elif which == "wQ":
                L(w_q, mwq_d[h].rearrange("(c p) n -> p c n", p=128), "wQ")
            elif which == "wK":
                L(w_k, mwk_d[h].rearrange("(c p) n -> p c n", p=128), "wK")
            elif which == "wV":
                sc.op("pool", lambda e: e.dma_start(out=w_v, in_=mwv_d[h].rearrange("(c p) n -> p c n", p=128)), writes=["wV", "wVb"], dma=True)
            elif which == "wO":
                L(w_out, mwout_d[cs, :].rearrange("(c p) n -> p c n", p=128), "wO")

        ob16 = outst[:].bitcast(BF16)
        HN2 = [hn[:], ob16[:, 0:512]]; HNK = ["hn", "hnB"]
        T12 = [t1[:], ob16[:, 512:1024].rearrange("p (c n) -> p c n", c=4)]; T1K = ["t1", "t1B"]
        YT2 = [yT[:], ob16[:, 1024:1536].rearrange("p (c n) -> p c n", c=4)]; YTK = ["yT", "yTB"]
        xcT2 = [xcT, xcTb]; Abf2 = [Abf, Abfb]; hh2 = [hh, hhB]
        XK = ["xcT", "xcTb"]; AK = ["Abf", "AbfB"]; HK = ["hh", "hhB"]

        def m_S2(h, tt, par, pf):
            tsl = slice(tt * T, (tt + 1) * T)
            ukey = "uT%d" % tt
            xcT_ = xcT2[par]; Abf_ = Abf2[par]; xk = XK[par]; ak = AK[par]
            if tt == 0:
                sc.op("pool", lambda e: e.memset(xm3[:, :, 0:3], 0.0), writes=XMF, cost=0.2)
            for fc in range(4):
                pb, pk = nextP()
                mm_group(pb[:], [(w_xm[:, kc, fc * 128:(fc + 1) * 128], uT[:, kc, tsl]) for kc in range(8)], ["wA", ukey], pk)
                sc.op("act", lambda e, pb=pb, fc=fc: e.activation(out=xm3[:, fc, 3:515], in_=pb[:], func=AF.Copy),
                      writes=[pk, "xmf%d" % fc])
                sc.op("act", lambda e, pb=pb, fc=fc: e.activation(out=xmT[:, fc, :], in_=pb[:], func=AF.Copy),
                      writes=[pk, "xmT"])
                cw = lambda j, fc=fc: mconv[:, h * 4 + fc, j:j + 1]
                sc.op("act", lambda e, fc=fc, cw=cw: e.activation(out=acc[:], in_=xm3[:, fc, 0:512], func=AF.Identity,
                                                                  scale=cw(0), bias=cw(4)),
                      reads=["xmf%d" % fc, "mconv"], writes=["acc"])
                for j in range(1, 4):
                    sc.op("dve", lambda e, fc=fc, j=j, cw=cw: e.scalar_tensor_tensor(
                        out=acc[:], in0=xm3[:, fc, j:j + 512], scalar=cw(j), in1=acc[:], op0=ALU.mult, op1=ALU.add),
                        reads=["xmf%d" % fc, "mconv", "acc"], writes=["acc"])
                sc.op("act", lambda e, fc=fc: e.activation(out=xcT_[:, fc, :], in_=acc[:], func=AF.Silu),
                      reads=["acc"], writes=[xk])
                sc.op("pool", lambda e, fc=fc: e.tensor_copy(out=xm3[:, fc, 0:3], in_=xm3[:, fc, 512:515]),
                      reads=["xmf%d" % fc], writes=["xmf%d" % fc])
            sc.mark("S2xc")
            pf("wA")
            for fc in range(4):
                pb, pk = nextP()
                mm_group(pb[:], [(w_z[:, kc, fc * 128:(fc + 1) * 128], uT[:, kc, tsl]) for kc in range(8)], ["wB", ukey], pk)
                sc.op("act", lambda e, pb=pb, fc=fc: e.activation(out=Abf_[:, fc, :], in_=pb[:], func=AF.Silu), writes=[pk, ak])
            pf("wB")
            sc.wait("osb_free")
            for sub in range(4):
                pb, pk = nextP()
                tk = slice(tt * T + sub * 128, tt * T + (sub + 1) * 128)
                mm_group(pb[:], [(uT[:, kc, tk], w_o[:, kc, :]) for kc in range(8)], ["wC", ukey], pk)
                sc.op("act", lambda e, pb=pb, sub=sub: e.activation(out=osb[:, sub, :], in_=pb[:], func=AF.Sigmoid), writes=[pk, "osb"])
            pf("wC")

        FB = [(NB, "ps_NB"), (CB[0], "ps_C0"), (CB[1], "ps_C1")]
        frot = [0]

        def nextF():
            i = frot[0]; frot[0] = (i + 1) % 3
            return FB[i]

        def m_S3(h, tt, par, pf):
            xcT_ = xcT2[par]; Abf_ = Abf2[par]; xk = XK[par]; ak = AK[par]
            sc.wait("Fdone")
            sc.wait("S2xc")
            for m2 in range(2):
                pb, pk = nextF()
                mm_group(pb[:], [(w_q[:, kc, m2 * 128:(m2 + 1) * 128], xcT_[:, kc, :]) for kc in range(4)], ["wQ", xk], pk)
                sc.op("act", lambda e, pb=pb, m2=m2: e.activation(out=qT[:, m2, :], in_=pb[:], func=AF.Copy, scale=DK ** -0.5),
                      writes=[pk, "qT"])
            for m2 in range(2):
                pb, pk = nextF()
                mm_group(pb[:], [(w_k[:, kc, m2 * 128:(m2 + 1) * 128], xcT_[:, kc, :]) for kc in range(4)], ["wK", xk], pk)
                sc.op("act", lambda e, pb=pb, m2=m2: e.activation(out=kT[:, m2, :], in_=pb[:], func=AF.Copy), writes=[pk, "kT"])
            for sub in range(4):
                pb, pk = nextF()
                mm_group(pb[:], [(xmT[:, kc, sub * 128:(sub + 1) * 128], w_v[:, kc, :]) for kc in range(4)], ["wV", "xmT"], pk)
                sc.op("act", lambda e, pb=pb, sub=sub: e.activation(out=vsb[:, sub, :], in_=pb[:], func=AF.Copy), writes=[pk, "vsb"])
            pf("wQ"); pf("wK"); pf("wV")
            for sub in range(4):
                c = tt * 4 + sub
                tbv = SD[:].bitcast(BF16)[:, 0:256].rearrange("p (c n) -> p c n", c=2)

                def tfn(e, tbv=tbv, sub=sub):
                    ins = None
                    for m2 in range(2):
                        ins = e.transpose(tbv[:, m2, :], kT[:, m2, sub * 128:(sub + 1) * 128], ident[:])
                    return ins
                sc.op("pe", tfn, reads=["kT", "ident"], writes=["ps_SD"], cost=0.15)
                sc.op("dve", lambda e, sub=sub, c=c: e.tensor_scalar(
                    out=kw[:, sub, :], in0=SD[:].bitcast(BF16)[:, 0:256], scalar1=cols[:, 1, c, h:h + 1], scalar2=None, op0=ALU.mult),
                    reads=["cols"], writes=["ps_SD", "kw"], cost=0.35)

        def m_CB(h, tt, par, pf):
            xcT_ = xcT2[par]; Abf_ = Abf2[par]; xk = XK[par]; ak = AK[par]
            def ab_compute():
                for fc in range(4):
                    f = h * 4 + fc
                    sc.op("dve", lambda e, fc=fc, f=f: e.scalar_tensor_tensor(
                        out=xcT_[:, fc, :], in0=xcT_[:, fc, :], scalar=mskip[:, f:f + 1], in1=Abf_[:, fc, :], op0=ALU.mult, op1=ALU.mult),
                        reads=[xk, "mskip", ak], writes=[xk], cost=0.5)
                    sc.op("act", lambda e, fc=fc, f=f: e.activation(out=Abf_[:, fc, :], in_=Abf_[:, fc, :], func=AF.Copy, scale=mnw[:, f:f + 1]),
                          reads=[ak, "mnw"], writes=[ak])

            def front(sub):
                c = tt * 4 + sub
                csl = slice(sub * 128, (sub + 1) * 128)
                hh_ = hh2[sub % 2]; hk = HK[sub % 2]
                dec = cols[:, 0, c, h:h + 1]
                ebc = cols[:, 1, c, h:h + 1]
                emc = cols[:, 2, c, h:h + 1]
                dprev = dlast[:, c, h:h + 1]
                dcur = dlast[:, c + 1, h:h + 1]
                skf = "smallF%d" % sub
                for m2 in range(2):
                    mm_group(CB[m2][:], [(kw[:, sub, m2 * 128:(m2 + 1) * 128], vsb[:, sub, :])], ["kw", "vsb"], "ps_C%d" % m2)
                for m2 in range(2):
                    mm_group(SD[:, 136 + m2:137 + m2], [(kw[:, sub, m2 * 128:(m2 + 1) * 128], onecol[:])], ["kw", "onecol"], "ps_SD")
                for m2 in range(2):
                    sc.op("dve", lambda e, m2=m2, dprev=dprev: e.scalar_tensor_tensor(
                        out=Chat[:, m2, :], in0=Chat[:, m2, :], scalar=dprev, in1=CB[m2][:], op0=ALU.mult, op1=ALU.add),
                        reads=["dlast", "Chat0", "Chat1"], writes=["ps_C%d" % m2, "Chat0", "Chat1"])
                sc.op("dve", lambda e, dprev=dprev: e.scalar_tensor_tensor(
                    out=nhat[:], in0=nhat[:], scalar=dprev, in1=SD[:, 136:138], op0=ALU.mult, op1=ALU.add),
                    reads=["dlast", "nhat"], writes=["ps_SD", "nhat"], cost=0.2)
                mm_group(SD[:, 0:128], [(kT[:, m2, csl], qT[:, m2, csl]) for m2 in range(2)], ["kT", "qT"], "ps_SD")
                sc.op("dve", lambda e, ebc=ebc: e.scalar_tensor_tensor(out=SmT[:], in0=SD[:, 0:128], scalar=ebc, in1=mask01[:],
                                                                       op0=ALU.mult, op1=ALU.mult),
                      reads=["cols", "mask01"], writes=["ps_SD", "SmT"], cost=0.35)
                inter = (c != 0)
                mm_group(NB[:], [(SmT[:], vsb[:, sub, :])] + ([(qT[:, m2, csl], Cbf[:, m2, :]) for m2 in range(2)] if inter else []),
                         ["SmT", "vsb", "qT"] + (["Cbf"] if inter else []), "ps_NB")
                mm_group(SD[:, 128:129], [(SmT[:], onecol[:])] + ([(qT[:, m2, csl], nbf16[:, m2:m2 + 1]) for m2 in range(2)] if inter else []),
                         ["SmT", "onecol", "qT"] + (["nbf16"] if inter else []), "ps_SD")
                sc.op("act", lambda e, dcur=dcur: e.activation(out=Cbf[:].rearrange("p a b -> p (a b)"),
                                                               in_=Chat[:].rearrange("p a b -> p (a b)"), func=AF.Copy, scale=dcur),
                      reads=["Chat0", "Chat1", "dlast"], writes=["Cbf"], cost=1.25)
                sc.op("act", lambda e, dcur=dcur: e.activation(out=nbf16[:], in_=nhat[:], func=AF.Copy, scale=dcur),
                      reads=["nhat", "dlast"], writes=["nbf16"], cost=0.3)
                s0 = small[:, sub * 8 + 0:sub * 8 + 1]; s1 = small[:, sub * 8 + 1:sub * 8 + 2]; s2 = small[:, sub * 8 + 2:sub * 8 + 3]
                sc.op("dve", lambda e: e.tensor_scalar(out=s0, in0=SD[:, 128:129], scalar1=-1.0, scalar2=None, op0=ALU.mult),
                      writes=["ps_SD", skf], cost=0.2)
                sc.op("dve", lambda e, emc=emc: e.scalar_tensor_tensor(out=s1, in0=SD[:, 128:129], scalar=emc, in1=s0, op0=ALU.max, op1=ALU.max),
                      reads=[skf, "cols"], writes=["ps_SD", skf], cost=0.2)
                sc.op("dve", lambda e: e.reciprocal(out=s2, in_=s1), reads=[skf], writes=[skf], cost=0.2)
                sc.op("dve", lambda e, sub=sub, hh_=hh_: e.scalar_tensor_tensor(out=hh_[:], in0=NB[:], scalar=s2, in1=osb[:, sub, :],
                                                                                  op0=ALU.mult, op1=ALU.mult),
                      reads=[skf, "osb"], writes=["ps_NB", hk])

            def back_a(sub):
                hh_ = hh2[sub % 2]; hk = HK[sub % 2]
                hn_ = HN2[sub % 2]; hnk = HNK[sub % 2]
                st6 = st6x[:, sub, :]; mv = mvx[:, sub, :]; skk = "smallK%d" % sub
                sc.op("dve", lambda e: e.bn_stats(out=st6, in_=hh_[:]), reads=[hk], writes=[skk])
                sc.op("dve", lambda e: e.bn_aggr(out=mv, in_=st6), reads=[skk], writes=[skk], cost=0.2)
                s3 = small[:, sub * 8 + 3:sub * 8 + 4]; s4 = small[:, sub * 8 + 4:sub * 8 + 5]; s5 = small[:, sub * 8 + 5:sub * 8 + 6]
                sc.op("pool", lambda e: e.tensor_scalar(out=s3, in0=mv[:, 1:2], scalar1=EPS, scalar2=None, op0=ALU.add),
                      reads=[skk], writes=[skk], cost=0.3)
                sc.op("pool", lambda e: e.tensor_tensor(out=s4, in0=s3, in1=nhalf[:], op=ALU.pow),
                      reads=[skk, "nhalf"], writes=[skk], cost=0.5)
                sc.op("dve", lambda e: e.scalar_tensor_tensor(out=s5, in0=mv[:, 0:1], scalar=-1.0, in1=s4, op0=ALU.mult, op1=ALU.mult),
                      reads=[skk], writes=[skk], cost=0.2)
                sc.op("act", lambda e: e.activation(out=hn_, in_=hh_[:], func=AF.Identity, bias=s5, scale=s4),
                      reads=[hk, skk], writes=[hnk])
                sc.mark("Khn%d" % sub)

            def back_b(sub):
                c = tt * 4 + sub
                csl = slice(sub * 128, (sub + 1) * 128)
                st = c
                hn_ = HN2[sub % 2]; hnk = HNK[sub % 2]
                t1_ = T12[sub % 2]; t1k = T1K[sub % 2]
                yT_ = YT2[sub % 2]; ytk = YTK[sub % 2]
                tb, tk_ = nextT()
                tbv = tb[:].bitcast(BF16)[:, 0:512].rearrange("p (c n) -> p c n", c=4)

                def tfn2(e, tbv=tbv):
                    ins = None
                    for fc in range(4):
                        ins = e.transpose(tbv[:, fc, :], hn_[:, fc * 128:(fc + 1) * 128], ident[:])
                    return ins
                sc.op("pe", tfn2, reads=[hnk, "ident"], writes=[tk_], cost=0.3)
                sc.mark("Ktr%d" % sub)
                sc.op("dve", lambda e, tbv=tbv, csl=csl: e.tensor_tensor(out=t1_, in0=tbv, in1=Abf_[:, :, csl], op=ALU.mult),
                      reads=[ak], writes=[tk_, t1k])
                sc.op("dve", lambda e, csl=csl: e.tensor_tensor(out=yT_, in0=t1_, in1=xcT_[:, :, csl], op=ALU.add),
                      reads=[t1k, xk], writes=[ytk], cost=0.45)
                for half in range(2):
                    pb, pk = nextT()
                    mm_group(pb[:], [(yT_[:, fc, :], w_out[:, fc, half * 512:(half + 1) * 512]) for fc in range(4)], [ytk, "wO"], pk)
                    sc.op("dve", lambda e, pb=pb, st=st, half=half: e.tensor_tensor(
                        out=X[:, st, half * 512:(half + 1) * 512], in0=X[:, st, half * 512:(half + 1) * 512], in1=pb[:], op=ALU.add),
                        reads=["X%d" % st], writes=[pk, "X%d" % st])

            def fstream():
                for sub in range(4):
                    if sub >= 2:
                        sc.wait("Khn%d" % (sub - 2))
                    front(sub)
                    sc.mark("F%d" % sub)
                sc.mark("osb_free")
                sc.mark("Fdone")

            def kastream():
                for sub in range(4):
                    sc.wait("F%d" % sub)
                    if sub >= 2:
                        sc.wait("Ktr%d" % (sub - 2))
                    back_a(sub)

            def kbstream():
                if h == NH - 1 and tt >= 1:
                    prepass_tile(1, tt - 1)
                ab_compute()
                for sub in range(4):
                    sc.wait("Khn%d" % sub)
                    back_b(sub)
                pf("wO")
            return sc.record(fstream), sc.record(kastream), sc.record(kbstream)

        def interleave(A, B):
            out = []
            ia = ib = 0
            na, nb = len(A), len(B)
            while ia < na or ib < nb:
                if ib >= nb or (ia < na and ia * nb <= ib * na):
                    out.append(A[ia]); ia += 1
                else:
                    out.append(B[ib]); ib += 1
            return out

        def rglru_load(g, which):
            L = lambda dst, src, key: sc.op("pool", lambda e: e.dma_start(out=dst, in_=src), writes=[key], dma=True)
            if which == "wA":
                L(w_xr, rwin_d[:, g * GW:(g + 1) * GW].rearrange("(c p) n -> p c n", p=128), "wA")
            elif which == "wB":
                L(w_g, rwin_d[:, RW + g * GW:RW + (g + 1) * GW].rearrange("(c p) n -> p c n", p=128), "wB")
            elif which == "wC":
                L(w_a, rwa_d[g].rearrange("(c p) n -> p c n", p=128), "wC")
                L(w_x, rwx_d[g].rearrange("(c p) n -> p c n", p=128), "wC")
            elif which == "wO":
                L(w_rout, rwout_d[g * GW:(g + 1) * GW, :].rearrange("(c p) n -> p c n", p=128), "wO")

        xr3 = xmf[:, 0:3 * 515].rearrange("p (c n) -> p c n", c=3)
        xcf = [acc, hh, None]
        KV = {0: (0, 1), 1: (0, 1, 2), 2: (1, 2)}

        def f32view(t):
            return t[:].rearrange("p a b -> p (a b)").bitcast(F32)

        RXC32 = [[acc[:], hh[:], Chat[:, 0, :]], [hhB[:], f32view(Cbf), f32view(kw)]]
        RXCK = [["acc", "hh", "Chat0"], ["hhB", "Cbf", "kw"]]
        RXCBF = [xmT, vsb]; RXCBFK = ["xmT", "vsb"]
        RGS = [xcT, xcTb]; RGSK = ["xcT", "xcTb"]

        def r_S2(g, tt, par, pf):
            tsl = slice(tt * T, (tt + 1) * T)
            ukey = "uT%d" % tt
            xc_bf = RXCBF[par]; xcbk = RXCBFK[par]
            gs_bf = RGS[par]; gsk = RGSK[par]
            xc32 = RXC32[par]; xck = RXCK[par]
            if tt == 0:
                sc.op("pool", lambda e: e.memset(xr3[:, :, 0:3], 0.0), writes=XMF, cost=0.2)
            for fc in range(3):
                f = g * 3 + fc
                pb, pk = nextP()
                mm_group(pb[:], [(w_xr[:, kc, fc * 128:(fc + 1) * 128], uT[:, kc, tsl]) for kc in range(8)], ["wA", ukey], pk)
                sc.op("act", lambda e, pb=pb, fc=fc: e.activation(out=xr3[:, fc, 3:515], in_=pb[:], func=AF.Copy),
                      writes=[pk, "xmf%d" % fc])
                cw = lambda j, f=f: rconv[:, f, j:j + 1]
                o32 = xc32[fc]
                sc.op("act", lambda e, fc=fc, cw=cw, o32=o32: e.activation(out=o32, in_=xr3[:, fc, 0:512], func=AF.Identity,
                                                                           scale=cw(0), bias=cw(4)),
                      reads=["xmf%d" % fc, "rconv"], writes=[xck[fc]])
                for j in range(1, 4):
                    sc.op("dve", lambda e, fc=fc, j=j, cw=cw, o32=o32: e.scalar_tensor_tensor(
                        out=o32, in0=xr3[:, fc, j:j + 512], scalar=cw(j), in1=o32, op0=ALU.mult, op1=ALU.add),
                        reads=["xmf%d" % fc, "rconv", xck[fc]], writes=[xck[fc]])
                sc.op("act", lambda e, fc=fc, o32=o32: e.activation(out=xc_bf[:, fc, :], in_=o32, func=AF.Copy), reads=[xck[fc]], writes=[xcbk])
                sc.op("pool", lambda e, fc=fc: e.tensor_copy(out=xr3[:, fc, 0:3], in_=xr3[:, fc, 512:515]),
                      reads=["xmf%d" % fc], writes=["xmf%d" % fc])
            pf("wA")
            for fc in range(3):
                pb, pk = nextP()
                mm_group(pb[:], [(w_g[:, kc, fc * 128:(fc + 1) * 128], uT[:, kc, tsl]) for kc in range(8)], ["wB", ukey], pk)
                sc.op("act", lambda e, pb=pb, fc=fc: e.activation(out=gs_bf[:, fc, :], in_=pb[:], func=AF.Silu), writes=[pk, gsk])
            pf("wB")

        YB = [Abf, Abfb]; YBK = ["Abf", "AbfB"]

        def r_Ga(g, tt, par, pf):
            xc_bf = RXCBF[par]; xcbk = RXCBFK[par]
            gs_bf = RGS[par]; gsk = RGSK[par]
            xc32 = RXC32[par]; xck = RXCK[par]
            y_bf = YB[par]; ybk = YBK[par]
            if tt == 0:
                sc.op("pool", lambda e: e.memset(hcar[:], 0.0), writes=["hcar"], cost=0.2)
            wVf = wV[:].bitcast(F32)
            if par == 0:
                Ta = [Chat[:, 1, :], f32view(qT), f32view(kT)]
                Tak = ["Chat1", "qT", "kT"]
                Tbs = [(outst[:, 0:512], "outA"), (outst[:, 512:1024], "outB")]
            else:
                Ta = [wQ[:].bitcast(F32), wK[:].bitcast(F32), wVf[:, 0:512]]
                Tak = ["wQ", "wK", "wV"]
                Tbs = [(wVf[:, 512:1024], "wVb"), (xn[:].bitcast(F32), "xn")]
            gbanks = []
            for fc in range(3):
                gb_r, gk_r = nextG()
                mm_group(gb_r[:], [(w_a[:, kc, fc * 128:(fc + 1) * 128], xc_bf[:, kc, :]) for kc in KV[fc]], ["wC", xcbk], gk_r)
                gbanks.append((gb_r, gk_r))
            for fc in range(3):
                f = g * 3 + fc
                gb_r, gk_r = gbanks[fc]
                sc.op("act", lambda e, gb_r=gb_r, f=f, fc=fc: e.activation(out=Ta[fc], in_=gb_r[:], func=AF.Tanh, bias=rbah[:, f:f + 1], scale=0.5),
                      reads=["rbah"], writes=[gk_r, Tak[fc]])
                sc.op("act", lambda e, f=f, fc=fc: e.activation(out=Ta[fc], in_=Ta[fc], func=AF.Exp, scale=rch[:, f:f + 1], bias=rch[:, f:f + 1]),
                      reads=[Tak[fc], "rch"], writes=[Tak[fc]])
            for fc in range(3):
                f = g * 3 + fc
                gb_i, gk_i = nextG()
                mm_group(gb_i[:], [(w_x[:, kc, fc * 128:(fc + 1) * 128], xc_bf[:, kc, :]) for kc in KV[fc]], ["wC", xcbk], gk_i)
                Tb, Tbk = Tbs[fc % 2]
                sc.op("act", lambda e, gb_i=gb_i, f=f, Tb=Tb: e.activation(out=Tb, in_=gb_i[:], func=AF.Tanh, bias=rbxh[:, f:f + 1], scale=0.5),
                      reads=["rbxh"], writes=[gk_i, Tbk])
                sc.op("dve", lambda e, fc=fc, Tb=Tb: e.scalar_tensor_tensor(out=xc32[fc], in0=Tb, scalar=1.0, in1=xc32[fc], op0=ALU.add, op1=ALU.mult),
                      reads=[Tbk, xck[fc]], writes=[xck[fc]])
            pf("wC")
            for fc in range(3):
                Tc, Tck = Tbs[(fc + 1) % 2]
                sc.op("act", lambda e, fc=fc, Tc=Tc: e.activation(out=Tc, in_=Ta[fc], func=AF.Square),
                      reads=[Tak[fc]], writes=[Tck])
                sc.op("act", lambda e, Tc=Tc: e.activation(out=Tc, in_=Tc, func=AF.Sqrt, bias=quartcol[:], scale=-0.25),
                      reads=[Tck, "quartcol"], writes=[Tck])
                sc.op("dve", lambda e, fc=fc, Tc=Tc: e.tensor_tensor(out=xc32[fc], in0=xc32[fc], in1=Tc, op=ALU.mult),
                      reads=[Tck, xck[fc]], writes=[xck[fc]])
                sc.op("dve", lambda e, fc=fc, Tc=Tc: e.tensor_tensor_scan(out=Tc, data0=Ta[fc], data1=xc32[fc], initial=hcar[:, fc:fc + 1],
                                                                          op0=ALU.mult, op1=ALU.add),
                      reads=[Tak[fc], xck[fc], "hcar"], writes=[Tck])
                sc.op("dve", lambda e, fc=fc, Tc=Tc: e.tensor_copy(out=hcar[:, fc:fc + 1], in_=Tc[:, 511:512]), reads=[Tck], writes=["hcar"], cost=0.2)
                sc.op("dve", lambda e, fc=fc, Tc=Tc: e.tensor_tensor(out=y_bf[:, fc, :], in0=Tc, in1=gs_bf[:, fc, :], op=ALU.mult),
                      reads=[Tck, gsk], writes=[ybk])

        def r_Gw(g, tt, par, final, pf):
            y_bf = YB[par]; ybk = YBK[par]
            ostage = f32view(osb)
            for sub in range(4):
                st = tt * 4 + sub
                ssl = slice(sub * 128, (sub + 1) * 128)
                for half in range(2):
                    pb, pk = nextT()
                    mm_group(pb[:], [(y_bf[:, fc, ssl], w_rout[:, fc, half * 512:(half + 1) * 512]) for fc in range(3)], [ybk, "wO"], pk)
                    sc.op("dve", lambda e, pb=pb, st=st, half=half: e.tensor_tensor(
                        out=X[:, st, half * 512:(half + 1) * 512], in0=X[:, st, half * 512:(half + 1) * 512], in1=pb[:], op=ALU.add),
                        reads=["X%d" % st], writes=[pk, "X%d" % st])
                if final is not None:
                    b = final
                    kx = "X%d" % st
                    sc.op("act", lambda e, st=st: e.activation(out=ostage, in_=X[:, st, :], func=AF.Square, accum_out=ss[:, st:st + 1]),
                          reads=[kx], writes=["osb", "ss%d" % st], cost=1.1)
                    sc.op("dve", lambda e, st=st: e.tensor_scalar(out=ss[:, st:st + 1], in0=ss[:, st:st + 1], scalar1=1.0 / D,
                                                                  scalar2=EPS, op0=ALU.mult, op1=ALU.add),
                          reads=["ss%d" % st], writes=["ss%d" % st], cost=0.2)
                    sc.op("pool", lambda e, st=st: e.tensor_tensor(out=rstd[:, st:st + 1], in0=ss[:, st:st + 1], in1=nhalf[:], op=ALU.pow),
                          reads=["ss%d" % st, "nhalf"], writes=["rstd%d" % st], cost=0.5)
                    sc.op("dve", lambda e, st=st: e.scalar_tensor_tensor(out=ostage, in0=X[:, st, :], scalar=rstd[:, st:st + 1],
                                                                         in1=fg[:], op0=ALU.mult, op1=ALU.mult),
                          reads=[kx, "rstd%d" % st, "fg"], writes=["osb"], cost=1.3)
                    sc.op("sp", lambda e, st=st, b=b: e.dma_start(out=out_d[b, st * 128:(st + 1) * 128, :], in_=ostage),
                          reads=["osb"], writes=["outd"], dma=True)
            pf("wO")

        for b in range(NSEQ):
            for st in range(NSUB):
                sc.op("sp", lambda e, b=b, st=st: e.dma_start(out=X[:, st, :], in_=x_d[b, st * 128:(st + 1) * 128, :]),
                      writes=["X%d" % st], dma=True)
            gbufs = mlstm_gates_prepass()
            sA = sc.record(lambda: prepass(0))
            sB = sc.record(lambda: [mlstm_gates_tile(tt, gbufs) for tt in range(NT)])
            sc.schedule([sA, sB])
            for w in ("wA", "wB", "wC", "wQ", "wK", "wV", "wO"):
                mlstm_load(0, w)
            units = [(h, tt) for h in range(NH) for tt in range(NT)]

            def pf_for(h, tt):
                if tt != NT - 1:
                    return lambda w: None
                if h + 1 < NH:
                    return lambda w, h=h: mlstm_load(h + 1, w)
                return lambda w: (rglru_load(0, w) if w in ("wA", "wB", "wC", "wO") else None)

            m_S2(units[0][0], units[0][1], 0, pf_for(*units[0]))
            m_S3(units[0][0], units[0][1], 0, pf_for(*units[0]))
            for i, (h, tt) in enumerate(units):
                par = i % 2
                pfu = pf_for(h, tt)
                fs, kas, kbs = m_CB(h, tt, par, pfu)
                streams = [fs, kas, kbs]
                if i + 1 < len(units):
                    h2, tt2 = units[i + 1]
                    streams.append(sc.record(lambda: m_S2(h2, tt2, 1 - par, pf_for(h2, tt2))))
                    streams.append(sc.record(lambda: m_S3(h2, tt2, 1 - par, pf_for(h2, tt2))))
                sc.schedule(streams)
            unitsR = [(g, tt) for g in range(NG) for tt in range(NT)]

            def pfr_for(g, tt):
                if tt == NT - 1 and g + 1 < NG:
                    return lambda w, g=g: rglru_load(g + 1, w)
                return lambda w: None

            sc.schedule([sc.record(lambda: r_S2(unitsR[0][0], unitsR[0][1], 0, pfr_for(*unitsR[0]))),
                         sc.record(lambda: prepass_tile(1, NT - 1))])
            nR = len(unitsR)
            for i in range(nR + 1):
                streams = []
                if i >= 1:
                    g0, tt0 = unitsR[i - 1]
                    streams.append(sc.record(lambda: r_Gw(g0, tt0, (i - 1) % 2, b if g0 == NG - 1 else None, pfr_for(g0, tt0))))
                if i < nR:
                    g1, tt1 = unitsR[i]
                    streams.append(sc.record(lambda: r_Ga(g1, tt1, i % 2, pfr_for(g1, tt1))))
                if i + 1 < nR:
                    g2, tt2 = unitsR[i + 1]
                    streams.append(sc.record(lambda: r_S2(g2, tt2, (i + 1) % 2, pfr_for(g2, tt2))))
                sc.schedule(streams)
        sc.finish("sp")
        block = es.enter_context(nc.Block())
        sc.emit(block)
    return nc


def _prep_shared(inp):
    f = lambda a: np.ascontiguousarray(np.asarray(a, dtype=np.float32))
    d = {}
    lng = f(inp["ln_g"])
    d["lng"] = f(lng.reshape(2, 8, 128).transpose(2, 0, 1).reshape(128, 16))
    d["fg_bc"] = f(np.broadcast_to(f(inp["final_g"])[None, :], (128, D)))
    mwin = f(inp["m_w_in"][0])
    d["m_w_in"] = mwin
    d["m_w_if"] = f(mwin[:, 3 * MI:3 * MI + 8])
    cw = f(inp["m_conv_w"][0])
    cb = f(inp["m_conv_b"][0])
    mc = np.concatenate([cw, cb[None, :]], 0)
    d["m_conv"] = f(mc.reshape(5, 16, 128).transpose(2, 1, 0).reshape(128, 80))
    d["m_nw"] = f(f(inp["m_norm_w"][0]).reshape(16, 128).T)
    d["m_skip"] = f(f(inp["m_skip"][0]).reshape(16, 128).T)
    d["m_bif"] = f(np.stack([f(inp["m_b_i"][0]), f(inp["m_b_f"][0])], 1))
    d["m_w_q"] = f(inp["m_w_q"][0]); d["m_w_k"] = f(inp["m_w_k"][0]); d["m_w_v"] = f(inp["m_w_v"][0])
    d["m_w_out"] = f(inp["m_w_out"][0])
    d["r_w_in"] = f(inp["r_w_in"][0])
    rw = f(inp["r_conv_w"][0]); rb = f(inp["r_conv_b"][0])
    rcv = np.concatenate([rw, rb[None, :]], 0)
    d["r_conv"] = f(rcv.reshape(5, 12, 128).transpose(2, 1, 0).reshape(128, 60))
    d["r_ba"] = f(f(inp["r_b_a"][0]).reshape(12, 128).T)
    d["r_bx"] = f(f(inp["r_b_x"][0]).reshape(12, 128).T)
    d["r_lam"] = f(f(inp["r_lam"][0]).reshape(12, 128).T)
    for nm, key in (("r_wa_pad", "r_w_a"), ("r_wx_pad", "r_w_x")):
        w = f(inp[key][0])
        pad = np.zeros((NG, GW, GW), np.float32)
        for g in range(NG):
            pad[g, 0:192, 0:192] = w[2 * g]
            pad[g, 192:384, 192:384] = w[2 * g + 1]
        d[nm] = pad
    d["r_w_out"] = f(inp["r_w_out"][0])
    return d


_CACHE = {}


def kernel(**inputs):
    x = np.ascontiguousarray(np.asarray(inputs["x"], dtype=np.float32))
    shared = _prep_shared(inputs)
    if "nc" not in _CACHE:
        _CACHE["nc"] = build_program()
    nc = _CACHE["nc"]
    in_maps = []
    for c in range(NCORES):
        m = dict(shared)
        m["x"] = np.ascontiguousarray(x[c * NSEQ:(c + 1) * NSEQ])
        in_maps.append(m)
    res = run_bass_kernel_spmd(nc, in_maps, core_ids=list(range(NCORES)))
    out = np.concatenate([np.asarray(r["out"], dtype=np.float32) for r in res.results], axis=0)
    return out
```
